# Optimizing a Trainium2 kernel written in Bass

```python
import jax, jax.numpy as jnp
from jax import lax
import numpy as np

D_MODEL = 1024
BATCH = 8
SEQ = 2048
DEPTH = 1
DEC_BATCH = 128
DEC_SEQ = 8
PAST_LEN = 16384
PAGE_SIZE = 128

D_A = D_MODEL // 2
HEAD_DIM = 128
N_HEADS = D_A // HEAD_DIM
D_B = D_MODEL // 2
CONV_W = 3
D_FF = 2816
CHUNK = 32
EPS = 1e-6
SPLIT_POINTS = (D_A, 2 * D_A, 3 * D_A, 4 * D_A, 4 * D_A + D_B, 4 * D_A + 2 * D_B,
                4 * D_A + 3 * D_B, 4 * D_A + 3 * D_B + D_MODEL)
N_IN = 4 * D_A + 3 * D_B + 2 * D_MODEL

kernel_name = "hgrn2_shortconv_gated_macaron_step"


def rmsnorm(x, g):
    xf = x.astype(jnp.float32)
    y = xf * lax.rsqrt(jnp.mean(xf * xf, axis=-1, keepdims=True) + EPS)
    return (y * g.astype(jnp.float32)).astype(x.dtype)


def swiglu(x, w1, w3, w2):
    return (jax.nn.silu(x @ w1) * (x @ w3)) @ w2


def hgrn2_chunked(q, k, v, logf, s0):
    bsz, L, H, dk = q.shape
    dv = v.shape[-1]
    C = CHUNK if L % CHUNK == 0 else L
    N = L // C
    q, k, v, logf = (t.reshape(bsz, N, C, H, t.shape[-1]) for t in (q, k, v, logf))
    b = jnp.cumsum(logf, axis=2)
    b_last = b[:, :, -1:]
    q_d = q * jnp.exp(b)
    k_d = k * jnp.exp(-b)
    scores = jnp.einsum('bnchk,bnshk->bnhcs', q_d, k_d)
    causal = jnp.tril(jnp.ones((C, C), dtype=bool))
    scores = jnp.where(causal, scores, 0.0)
    o_intra = jnp.einsum('bnhcs,bnshv->bnchv', scores, v)
    ds = jnp.einsum('bnchk,bnchv->bnhkv', k * jnp.exp(b_last - b), v)
    decay = jnp.exp(b_last[:, :, 0])

    def step(s, inp):
        dec, d = inp
        return dec[..., None] * s + d, s

    s_final, s_starts = lax.scan(step, s0, (jnp.moveaxis(decay, 1, 0), jnp.moveaxis(ds, 1, 0)))
    s_starts = jnp.moveaxis(s_starts, 0, 1)
    o_inter = jnp.einsum('bnchk,bnhkv->bnchv', q_d, s_starts)
    return (o_intra + o_inter).reshape(bsz, L, H, dv), s_final


def parallel_mixer(h, s_hgrn, s_conv, lb, w_in, conv_w, g_hgrn_out, w_a_out, w_b_out, w_o):
    bsz, L, _ = h.shape
    f32 = jnp.float32
    p = h @ w_in
    q, fz, iv, og, bg, cg, vv, ga, gb = jnp.split(p, SPLIT_POINTS, axis=-1)

    heads = lambda t: t.astype(f32).reshape(bsz, L, N_HEADS, HEAD_DIM)
    z = heads(fz)
    lbh = lb.astype(f32).reshape(N_HEADS, HEAD_DIM)
    f = lbh + (1.0 - lbh) * jax.nn.sigmoid(z)
    k_in = (1.0 - lbh) * jax.nn.sigmoid(-z)
    o, s_hgrn_new = hgrn2_chunked(heads(q), k_in, heads(iv), jnp.log(f), s_hgrn.astype(f32))
    o = o * lax.rsqrt(jnp.mean(o * o, axis=-1, keepdims=True) + EPS)
    o = o * g_hgrn_out.astype(f32).reshape(N_HEADS, HEAD_DIM)
    o = o.reshape(bsz, L, D_A) * jax.nn.silu(og.astype(f32))
    y_a = o.astype(h.dtype) @ w_a_out

    u = cg * vv
    up = jnp.concatenate([s_conv.astype(u.dtype), u], axis=1)
    conv = sum(conv_w[j] * up[:, j:j + L] for j in range(CONV_W))
    y_b = (bg * conv) @ w_b_out
    s_conv_new = up[:, L:]

    merged = jax.nn.sigmoid(ga) * y_a + jax.nn.sigmoid(gb) * y_b
    return merged @ w_o, s_hgrn_new.astype(s_hgrn.dtype), s_conv_new.astype(s_conv.dtype)


def trunk(x, st_h, st_c, lower_bound_logits, g_ffn1, w1_ffn1, w3_ffn1, w2_ffn1, g_mix, w_in,
          conv_w, g_hgrn_out, w_a_out, w_b_out, w_o, g_ffn2, w1_ffn2, w3_ffn2, w2_ffn2, g_final):
    lb_all = jnp.cumsum(jax.nn.softmax(lower_bound_logits.astype(jnp.float32), axis=0), axis=0)
    new_h, new_c = [], []
    for l in range(DEPTH):
        x = x + 0.5 * swiglu(rmsnorm(x, g_ffn1[l]), w1_ffn1[l], w3_ffn1[l], w2_ffn1[l])
        mix, sh, sc = parallel_mixer(rmsnorm(x, g_mix[l]), st_h[l], st_c[l], lb_all[l], w_in[l],
                                     conv_w[l], g_hgrn_out[l], w_a_out[l], w_b_out[l], w_o[l])
        x = x + mix
        x = x + 0.5 * swiglu(rmsnorm(x, g_ffn2[l]), w1_ffn2[l], w3_ffn2[l], w2_ffn2[l])
        new_h.append(sh)
        new_c.append(sc)
    return rmsnorm(x, g_final), jnp.stack(new_h), jnp.stack(new_c)


def setup_inputs(seed: int = 0) -> dict:
    key = jax.random.key(seed)
    ks = jax.random.split(key, 32)
    nrm = lambda k, shape, scale: jax.random.normal(k, shape, jnp.float32) * scale
    gain = lambda k, shape: 1.0 + 0.02 * jax.random.normal(k, shape, jnp.float32)
    return {
        "x_prompt": nrm(ks[0], (BATCH, SEQ, D_MODEL), 1.0),
        "x_sample": nrm(ks[1], (DEC_BATCH, DEC_SEQ, D_MODEL), 1.0),
        "state_hgrn": nrm(ks[2], (DEPTH, DEC_BATCH, N_HEADS, HEAD_DIM, HEAD_DIM), 0.5),
        "state_conv": nrm(ks[3], (DEPTH, DEC_BATCH, CONV_W - 1, D_B), 1.0),
        "lower_bound_logits": nrm(ks[4], (DEPTH + 1, D_A), 0.1),
        "g_ffn1": gain(ks[5], (DEPTH, D_MODEL)),
        "w1_ffn1": nrm(ks[6], (DEPTH, D_MODEL, D_FF), D_MODEL ** -0.5),
        "w3_ffn1": nrm(ks[7], (DEPTH, D_MODEL, D_FF), D_MODEL ** -0.5),
        "w2_ffn1": nrm(ks[8], (DEPTH, D_FF, D_MODEL), D_FF ** -0.5),
        "g_mix": gain(ks[9], (DEPTH, D_MODEL)),
        "w_in": nrm(ks[10], (DEPTH, D_MODEL, N_IN), D_MODEL ** -0.5),
        "conv_w": nrm(ks[11], (DEPTH, CONV_W, D_B), CONV_W ** -0.5),
        "g_hgrn_out": gain(ks[12], (DEPTH, D_A)),
        "w_a_out": nrm(ks[13], (DEPTH, D_A, D_MODEL), D_A ** -0.5),
        "w_b_out": nrm(ks[14], (DEPTH, D_B, D_MODEL), D_B ** -0.5),
        "w_o": nrm(ks[15], (DEPTH, D_MODEL, D_MODEL), D_MODEL ** -0.5),
        "g_ffn2": gain(ks[16], (DEPTH, D_MODEL)),
        "w1_ffn2": nrm(ks[17], (DEPTH, D_MODEL, D_FF), D_MODEL ** -0.5),
        "w3_ffn2": nrm(ks[18], (DEPTH, D_MODEL, D_FF), D_MODEL ** -0.5),
        "w2_ffn2": nrm(ks[19], (DEPTH, D_FF, D_MODEL), D_FF ** -0.5),
        "g_final": gain(ks[20], (D_MODEL,)),
    }


def reference(x_prompt, x_sample, state_hgrn, state_conv, lower_bound_logits, g_ffn1, w1_ffn1,
              w3_ffn1, w2_ffn1, g_mix, w_in, conv_w, g_hgrn_out, w_a_out, w_b_out, w_o, g_ffn2,
              w1_ffn2, w3_ffn2, w2_ffn2, g_final):
    weights = (lower_bound_logits, g_ffn1, w1_ffn1, w3_ffn1, w2_ffn1, g_mix, w_in, conv_w,
               g_hgrn_out, w_a_out, w_b_out, w_o, g_ffn2, w1_ffn2, w3_ffn2, w2_ffn2, g_final)
    n_prompt = x_prompt.shape[0]
    zero_h = jnp.zeros((DEPTH, n_prompt) + state_hgrn.shape[2:], state_hgrn.dtype)
    zero_c = jnp.zeros((DEPTH, n_prompt) + state_conv.shape[2:], state_conv.dtype)
    y_prompt, state_hgrn_prompt, state_conv_prompt = trunk(x_prompt, zero_h, zero_c, *weights)
    y_sample, state_hgrn_sample, state_conv_sample = trunk(x_sample, state_hgrn, state_conv, *weights)
    return (y_prompt, y_sample, state_hgrn_prompt, state_conv_prompt, state_hgrn_sample, state_conv_sample)
```

```python
import os
import numpy as np
from contextlib import ExitStack
import concourse.bass as bass
import concourse.mybir as mybir
from concourse.bass_utils import run_bass_kernel_spmd

F32 = mybir.dt.float32
F32R = mybir.dt.float32r
BF16 = mybir.dt.bfloat16
AF = mybir.ActivationFunctionType
ALU = mybir.AluOpType

NCORES = 8
P = 128
D = 1024
KD = 8
DFF = 2816
NJ = 22
NIN = 5632
TM = 1152
EPS = 1e-6
NSLOT = 6
FFN_GROUPS = [(0, 8), (8, 8), (16, 6)]
CQ, CF, CI, COG, CBG, CCG, CVV, CGA, CGB = 0, 512, 1024, 1536, 2048, 2560, 3072, 3584, 4608
PG1, PGM, PG2, PGF, PL0, PL1, PGH, PCW, NPAR = 0, 8, 16, 24, 32, 36, 40, 44, 56
C_ID, C_ONE, C_MP, C_MS, C_R64, C_RMIX, C_M2, C_CM2, NCST = 0, 128, 256, 384, 512, 1024, 1408, 1424, 1426

PASSES = [
    dict(T=1152, subs=[(0, 384), (384, 384), (768, 384)], npb=8, ptok0=0, sample=True),
    dict(T=1024, subs=[(0, 512), (512, 512)], npb=8, ptok0=1024, sample=False),
]


class Tok:
    __slots__ = ("sem", "val", "eng")

    def __init__(self, sem, val, eng):
        self.sem, self.val, self.eng = sem, val, eng


class Buf:
    __slots__ = ("name", "w", "r")

    def __init__(self, name=""):
        self.name, self.w, self.r = name, None, []


class Eng:
    def __init__(self, name, sem, is_pe=False):
        self.name, self.sem, self.is_pe = name, sem, is_pe
        self.count = 0
        self.ops = []
        self.seen = {}

    def wait(self, tok):
        if tok is None:
            return
        k = id(tok.sem)
        if self.seen.get(k, 0) >= tok.val:
            return
        self.seen[k] = tok.val
        sem, val = tok.sem, tok.val
        self.ops.append(lambda e: e.wait_ge(sem, val))


class DmaSem:
    def __init__(self, sem):
        self.sem, self.count = sem, 0


class Prog:
    def __init__(self, sems):
        self.pe = Eng("pe", sems["pe"], is_pe=True)
        self.act = Eng("act", sems["act"])
        self.dve = Eng("dve", sems["dve"])
        self.pool = Eng("pool", sems["pool"])
        self.sp = Eng("sp", sems["sp"])
        self.bufs = {}

    def B(self, *key):
        b = self.bufs.get(key)
        if b is None:
            b = self.bufs[key] = Buf(str(key))
        return b

    def _deps(self, eng, reads, writes):
        for b in reads:
            if b.w is not None and not (eng.is_pe and b.w.eng is eng):
                eng.wait(b.w)
        for b in writes:
            for t in b.r:
                if t.eng is not eng or not eng.is_pe:
                    eng.wait(t)
            if b.w is not None and (b.w.eng is not eng or not eng.is_pe):
                eng.wait(b.w)

    @staticmethod
    def _commit(tok, reads, writes):
        for b in reads:
            b.r.append(tok)
        for b in writes:
            b.w = tok
            b.r = []

    def op(self, eng, fn, reads=(), writes=()):
        self._deps(eng, reads, writes)
        eng.count += 1
        sem, tok = eng.sem, Tok(eng.sem, eng.count, eng)
        eng.ops.append(lambda e: fn(e).then_inc(sem, 1))
        self._commit(tok, reads, writes)
        return tok

    def group(self, eng, fns, reads=(), writes=()):
        self._deps(eng, reads, writes)
        eng.count += 1
        sem, tok = eng.sem, Tok(eng.sem, eng.count, eng)
        for fn in fns[:-1]:
            eng.ops.append(fn)
        last = fns[-1]
        eng.ops.append(lambda e: last(e).then_inc(sem, 1))
        self._commit(tok, reads, writes)
        return tok

    def dma(self, eng, dsem, fn, reads=(), writes=()):
        self._deps(eng, reads, writes)
        dsem.count += 16
        tok = Tok(dsem.sem, dsem.count, None)
        sem = dsem.sem
        eng.ops.append(lambda e: fn(e).then_inc(sem, 16))
        self._commit(tok, reads, writes)
        return tok

    def replay(self, block):
        def mk(eng):
            def body(e):
                for f in eng.ops:
                    f(e)
            return body
        block.tensor(mk(self.pe))
        block.scalar(mk(self.act))
        block.vector(mk(self.dve))
        block.gpsimd(mk(self.pool))
        block.sync(mk(self.sp))


class WRing:
    def __init__(self, pg, arena, dsems, plan):
        self.pg, self.arena, self.dsems, self.plan = pg, arena, dsems, plan
        self.bufs = [Buf("wslot%d" % i) for i in range(NSLOT)]
        self.next_load = 0
        self.next_get = 0
        self.released = 0
        self.pump()

    def _view(self, idx):
        _, _, shape = self.plan[idx]
        n = shape[1] * shape[2]
        return self.arena[:, idx % NSLOT, 0:n].rearrange("p (k n) -> p k n", k=shape[1])

    def pump(self):
        while self.next_load < len(self.plan) and self.next_load < self.released + NSLOT:
            idx = self.next_load
            view = self._view(idx)
            src = self.plan[idx][1]
            self.pg.dma(self.pg.sp, self.dsems[idx % NSLOT],
                        lambda e, view=view, src=src: e.dma_start(out=view, in_=src),
                        writes=[self.bufs[idx % NSLOT]])
            self.next_load += 1

    def get(self, key):
        idx = self.next_get
        assert self.plan[idx][0] == key, (self.plan[idx][0], key)
        assert idx < self.next_load, "weight tile not issued (ring too small)"
        self.next_get += 1
        return self._view(idx), self.bufs[idx % NSLOT]

    def release(self, n=1):
        self.released += n
        self.pump()


def wtile(w2d, row0, nk, col0, ncol=128):
    return w2d[row0:row0 + nk * 128, col0:col0 + ncol].rearrange("(k p) n -> p k n", p=128)


def build_program(stage=99):
    nc = bass.Bass("TRN2", target_bir_lowering=False)
    nc.dge_precook = False

    def din(name, shape, dt=F32):
        return nc.dram_tensor(name, shape, dt, kind="ExternalInput").ap()

    def dout(name, shape):
        return nc.dram_tensor(name, shape, F32, kind="ExternalOutput").ap()

    xp_d = din("xp", [2048, D])
    xs_d = din("xs", [128, D])
    sh_d = din("sh", [16, 4, 128, 128], F32R)
    sc_d = din("sc", [32, 512])
    par_d = din("par", [128, NPAR])
    cst_d = din("cst", [128, NCST])
    w1a_d = din("w1a", [D, DFF], F32R)
    w3a_d = din("w3a", [D, DFF], F32R)
    w2a_d = din("w2a", [DFF, D], F32R)
    win_d = din("win", [D, NIN], F32R)
    wab_d = din("wab", [1024, D], F32R)
    wo_d = din("wo", [D, D], F32R)
    w1b_d = din("w1b", [D, DFF], F32R)
    w3b_d = din("w3b", [D, DFF], F32R)
    w2b_d = din("w2b", [DFF, D], F32R)
    yp_d = dout("yp", [2048, D])
    ys_d = dout("ys", [128, D])
    shp_d = dout("shp", [4, 128, 128])
    scp_d = dout("scp", [2, 512])
    shs_d = dout("shs", [16, 4, 128, 128])
    scs_d = dout("scs", [32, 512])

    def ffn_plan(tag, w1, w3, w2):
        pl = []
        for g, (j0, nj) in enumerate(FFN_GROUPS):
            for j in range(j0, j0 + nj):
                pl.append(((tag, "w1", j), wtile(w1, 0, 8, j * 128), (128, 8, 128)))
                pl.append(((tag, "w3", j), wtile(w3, 0, 8, j * 128), (128, 8, 128)))
            for m in range(8):
                pl.append(((tag, "w2", g, m), wtile(w2, j0 * 128, nj, m * 128), (128, nj, 128)))
        return pl

    def mixer_plan():
        pl = []
        for kp in range(4):
            pl.append((("wv", kp), wtile(win_d, kp * 256, 2, CI, 512), (128, 2, 512)))
        if stage < 2.2:
            return pl
        for h in range(4):
            pl.append((("wq", h), wtile(win_d, 0, 8, CQ + h * 128), (128, 8, 128)))
            pl.append((("wf", h), wtile(win_d, 0, 8, CF + h * 128), (128, 8, 128)))
            pl.append((("wog", h), wtile(win_d, 0, 8, COG + h * 128), (128, 8, 128)))
        if stage < 2.3:
            return pl
        for c in range(4):
            pl.append((("wbg", c), wtile(win_d, 0, 8, CBG + c * 128), (128, 8, 128)))
            pl.append((("wcg", c), wtile(win_d, 0, 8, CCG + c * 128), (128, 8, 128)))
            pl.append((("wvv", c), wtile(win_d, 0, 8, CVV + c * 128), (128, 8, 128)))
        if stage < 2.4:
            return pl
        for qt in range(4):
            for mm in range(2):
                m = qt * 2 + mm
                pl.append((("wga", m), wtile(win_d, 0, 8, CGA + m * 128), (128, 8, 128)))
                pl.append((("wgb", m), wtile(win_d, 0, 8, CGB + m * 128), (128, 8, 128)))
                pl.append((("wab", m), wtile(wab_d, 0, 8, m * 128), (128, 8, 128)))
            for mo in range(8):
                pl.append((("wo", qt, mo), wtile(wo_d, qt * 256, 2, mo * 128), (128, 2, 128)))
        return pl

    plan = []
    for ip in range(len(PASSES)):
        if stage >= 1:
            plan += ffn_plan("a", w1a_d, w3a_d, w2a_d)
        if stage >= 2:
            plan += mixer_plan()
        if stage >= 3:
            plan += ffn_plan("b", w1b_d, w3b_d, w2b_d)

    es = ExitStack()
    with es:
        def sb(name, shape, dt):
            return es.enter_context(nc.sbuf_tensor("sb_" + name, shape, dt))

        def newsem(name):
            return es.enter_context(nc.semaphore(name))

        xT = sb("xT", [128, KD, TM], F32)
        hT = sb("hT", [128, KD, TM], F32R)
        WB = sb("WB", [128, 10, TM], F32R)
        arena = sb("arena", [128, NSLOT, 1024], F32R)
        vtok = sb("vtok", [128, 9, 512], BF16)
        NTMP = 8
        tmp = [sb("tmp%d" % i, [128, 520], F32) for i in range(NTMP)]
        xst = [sb("xst%d" % i, [128, D], F32) for i in range(2)]
        s0t = sb("s0t", [128, 8, 128], F32R)
        snw = sb("snw", [128, 8, 128], F32)
        vblk = sb("vblk", [128, 8, 128], BF16)
        S = sb("S", [128, 4, 128], F32)
        kdb = [sb("kdb%d" % i, [128, 512], BF16) for i in range(2)]
        qdb = [sb("qdb%d" % i, [128, 512], BF16) for i in range(2)]
        kkb = [sb("kkb%d" % i, [128, 512], BF16) for i in range(2)]
        cm2b = sb("cm2b", [128, 2], BF16)
        osq = sb("osq", [128, 512], F32R)
        rrow = sb("rrow", [128, 4], F32R)
        ms4 = sb("ms4", [128, 8], F32)
        identr = sb("identr", [128, 128], F32R)
        par = sb("par", [128, NPAR], F32)
        cst = sb("cst", [128, NCST], F32)
        drv = sb("drv", [128, 16], F32)
        identb = sb("identb", [128, 128], BF16)
        onesr = sb("onesr", [128, 128], F32R)
        maskP4 = sb("maskP4", [128, 512], BF16)
        maskMix = sb("maskMix", [128, 384], BF16)
        m2b = sb("m2b", [128, 16], BF16)
        neghalf = sb("neghalf", [128, 8], F32)
        ucar = sb("ucar", [128, 4, 2], F32)
        usmp = sb("usmp", [128, 4, 16, 10], F32)
        ps = es.enter_context(nc.psum_tensor("ps", [128, 8, 512], F32))
        scst = tmp[6][0:32, 0:512]
        kkT = [xst[0][:, j_ * 256:(j_ + 1) * 256].bitcast(BF16) for j_ in range(2)]
        scmb = [xst[0][:, 512 + j_ * 256:512 + (j_ + 1) * 256].bitcast(BF16) for j_ in range(2)]
        sco = tmp[7][0:32, 0:512]

        sems = {n: newsem("s_" + n) for n in ["pe", "act", "dve", "pool", "sp"]}
        pg = Prog(sems)
        B = pg.B
        wsems = [DmaSem(newsem("w%d" % i)) for i in range(NSLOT)]
        NXS = 5
        xsem = [[DmaSem(newsem("xl%d_%d" % (i, hf_))) for hf_ in range(2)] for i in range(NXS)]
        ysem = [DmaSem(newsem("ys%d" % i)) for i in range(2)]
        msem = DmaSem(newsem("misc"))
        s0sem = DmaSem(newsem("s0l"))
        snsem = DmaSem(newsem("sns"))
        s0sem2 = [DmaSem(newsem("s0l%d" % i)) for i in range(2)]
        snsem2 = [DmaSem(newsem("sns%d" % i)) for i in range(2)]
        osem = DmaSem(newsem("outs"))
        block = es.enter_context(nc.Block())

        pe, act, dve, pool, sp = pg.pe, pg.act, pg.dve, pg.pool, pg.sp
        PSB = [B("ps", i) for i in range(8)]
        TB = [B("tmp", i) for i in range(NTMP)]
        ident = cst[:, C_ID:C_ID + 128]
        CST = B("cst")

        def pcol(c):
            return par[:, c:c + 1]

        pg.dma(sp, msem, lambda e: e.dma_start(out=par[:], in_=par_d), writes=[B("par")])
        pg.dma(sp, msem, lambda e: e.dma_start(out=cst[:], in_=cst_d), writes=[CST])
        t_last = pg.dma(sp, msem, lambda e: e.dma_start(out=scst, in_=sc_d), writes=[TB[6]])
        for b_ in [B("par"), CST, TB[6]]:
            b_.w = t_last
        W = None

        pg.op(pool, lambda e: e.memset(neghalf[:], -0.5), writes=[B("neghalf")])
        pg.op(pool, lambda e: e.memset(S[:], 0.0), writes=[B("S")])
        pg.op(pool, lambda e: e.memset(ucar[:], 0.0), writes=[B("ucar")])
        pg.op(dve, lambda e: e.tensor_copy(identb[:], ident), reads=[CST], writes=[B("identb")])
        pg.op(dve, lambda e: e.tensor_copy(onesr[:], cst[:, C_ONE:C_ONE + 128]), reads=[CST], writes=[B("onesr")])
        pg.op(dve, lambda e: e.tensor_copy(identr[:], ident), reads=[CST], writes=[B("identr")])
        for i_ in range(4):
            pg.op(dve, lambda e, i_=i_: e.tensor_copy(maskP4[:, i_ * 128:(i_ + 1) * 128], cst[:, C_MP:C_MP + 128]), reads=[CST], writes=[B("masks")])
        for i_ in range(3):
            c_ = C_MP if i_ < 2 else C_MS
            pg.op(dve, lambda e, i_=i_, c_=c_: e.tensor_copy(maskMix[:, i_ * 128:(i_ + 1) * 128], cst[:, c_:c_ + 128]), reads=[CST], writes=[B("masks")])
        pg.op(dve, lambda e: e.tensor_copy(m2b[:], cst[:, C_M2:C_M2 + 16]), reads=[CST], writes=[B("m2b")])
        pg.op(dve, lambda e: e.tensor_copy(cm2b[:], cst[:, C_CM2:C_CM2 + 2]), reads=[CST], writes=[B("cm2b")])
        pg.op(dve, lambda e: e.tensor_tensor(out=drv[:, 8:12], in0=par[:, PL0:PL0 + 4], in1=par[:, PL1:PL1 + 4], op=ALU.subtract),
              reads=[B("par")], writes=[B("drv")])
        pg.op(act, lambda e: e.activation(out=drv[:, 12:16], in_=drv[:, 8:12], func=AF.Tanh, scale=0.5),
              reads=[B("drv")], writes=[B("drv")])
        pg.op(dve, lambda e: e.tensor_scalar(out=drv[:, 0:4], in0=drv[:, 12:16], scalar1=-0.25, scalar2=0.25, op0=ALU.mult, op1=ALU.add),
              reads=[B("drv")], writes=[B("drv")])
        pg.op(dve, lambda e: e.tensor_scalar(out=drv[:, 4:8], in0=drv[:, 0:4], scalar1=-1.0, scalar2=None, op0=ALU.mult),
              reads=[B("drv")], writes=[B("drv")])
        pg.op(dve, lambda e: e.tensor_scalar(out=drv[:, 8:12], in0=drv[:, 0:4], scalar1=-1.0, scalar2=1.0, op0=ALU.mult, op1=ALU.add),
              reads=[B("drv")], writes=[B("drv")])
        DRV = B("drv")

        def load_sample_conv_state():
            fns = []
            for c in range(4):
                fns.append(lambda e, c=c: e.transpose(ps[:, 7, c * 32:(c + 1) * 32], scst[:, c * 128:(c + 1) * 128], cst[0:32, C_ID:C_ID + 32]))
            pg.group(pe, fns, reads=[TB[6], CST], writes=[PSB[7]])
            for c in range(4):
                pg.op(dve, lambda e, c=c: e.tensor_copy(usmp[:, c, :, 0:2], ps[:, 7, c * 32:(c + 1) * 32].rearrange("p (i t) -> p i t", t=2)),
                      reads=[PSB[7]], writes=[B("usmp", c)])

        def xslot(sl, hf):
            if sl < 2:
                return xst[sl][:, hf * 512:(hf + 1) * 512], B("xst", sl, hf)
            a_ = 2 * (sl - 2) + hf
            return tmp[a_][:, 0:512], TB[a_]

        def load_x(ip):
            pa = PASSES[ip]
            nblk = pa["T"] // 128
            for b in range(nblk):
                sl = b % NXS
                for hf in range(2):
                    ap_, buf_ = xslot(sl, hf)
                    if b < pa["npb"]:
                        r0 = pa["ptok0"] + b * 128
                        src = xp_d[r0:r0 + 128, hf * 512:(hf + 1) * 512]
                    else:
                        src = xs_d[:, hf * 512:(hf + 1) * 512]
                    pg.dma(sp, xsem[sl][hf], lambda e, ap_=ap_, src=src: e.dma_start(out=ap_, in_=src), writes=[buf_])
                for hf in range(2):
                    ap_, buf_ = xslot(sl, hf)
                    bank = 4 + 2 * (b % 2) + hf
                    fns = [lambda e, k=k, ap_=ap_, bank=bank: e.transpose(ps[:, bank, (k % 4) * 128:(k % 4 + 1) * 128],
                                                                          ap_[:, (k % 4) * 128:(k % 4 + 1) * 128], ident)
                           for k in range(hf * 4, hf * 4 + 4)]
                    pg.group(pe, fns, reads=[buf_, CST], writes=[PSB[bank]])
                    dst = xT[:, hf * 4:hf * 4 + 4, b * 128:(b + 1) * 128]
                    src_ps = ps[:, bank, :].rearrange("p (k n) -> p k n", k=4)
                    wr = [B("xT", k, b) for k in range(hf * 4, hf * 4 + 4)]
                    if hf == 0:
                        pg.op(act, lambda e, dst=dst, src_ps=src_ps: e.activation(out=dst, in_=src_ps, func=AF.Copy),
                              reads=[PSB[bank]], writes=wr)
                    else:
                        pg.op(dve, lambda e, dst=dst, src_ps=src_ps: e.tensor_copy(dst, src_ps),
                              reads=[PSB[bank]], writes=wr)

        def mm(out, lhsT, rhs, start, stop, skip=False):
            return lambda e: e.matmul(out, lhsT=lhsT, rhs=rhs, start=start, stop=stop, skip_group_check=skip)

        def proj(wv_, wb_, bank, si, c0, n):
            pg.group(pe, [mm(ps[:, bank, 0:n], wv_[:, k, :], hT[:, k, c0:c0 + n], k == 0, k == KD - 1) for k in range(KD)],
                     reads=[wb_, B("hT", si)], writes=[PSB[bank]])

        def rstd_bcast(sqs, sqbufs, n, inv_dim, ss_bank, out_bank, split=False):
            nb = n // 128
            nk = len(sqs)
            if nk > 1:
                pg.group(pe, [mm(ps[:, ss_bank, 0:n], onesr[:], sqs[k], k == 0, k == nk - 1) for k in range(nk)],
                         reads=list(sqbufs) + [B("onesr")], writes=[PSB[ss_bank]])
                pg.op(act, lambda e: e.activation(out=osq[0:1, 0:n], in_=ps[0:1, ss_bank, 0:n], func=AF.Copy),
                      reads=[PSB[ss_bank]], writes=[B("osq")])
                pg.group(pe, [mm(ps[:, ss_bank, 2 * blk:2 * blk + 2], osq[0:1, blk * 128:(blk + 1) * 128], onesr[0:1, 0:2], True, True, True)
                              for blk in range(nb)], reads=[B("osq"), B("onesr")], writes=[PSB[ss_bank]])
            else:
                fns = []
                for blk in range(nb):
                    for k in range(nk):
                        fns.append(mm(ps[:, ss_bank, 2 * blk:2 * blk + 2], sqs[k][:, blk * 128:(blk + 1) * 128], onesr[:, 0:2], k == 0, k == nk - 1, True))
                pg.group(pe, fns, reads=list(sqbufs) + [B("onesr")], writes=[PSB[ss_bank]])
            pg.op(dve, lambda e: e.tensor_scalar(out=ms4[:, 0:nb], in0=ps[:, ss_bank, 0:2 * nb].rearrange("p (b t) -> p b t", t=2)[:, :, 0],
                                                  scalar1=inv_dim, scalar2=EPS, op0=ALU.mult, op1=ALU.add), reads=[PSB[ss_bank]], writes=[B("ms4")])
            pg.op(pool, lambda e: e.tensor_tensor(out=rrow[:, 0:nb], in0=ms4[:, 0:nb], in1=neghalf[:, 0:nb], op=ALU.pow),
                  reads=[B("ms4"), B("neghalf")], writes=[B("rrow")])
            if not split:
                rstd_b(n, out_bank)

        def rstd_b(n, out_bank):
            nb = n // 128
            pg.group(pe, [mm(ps[:, out_bank, blk * 128:(blk + 1) * 128], rrow[:, blk:blk + 1].broadcast_to([128, 128]), identr[:], True, True, True)
                          for blk in range(nb)],
                     reads=[B("rrow"), B("identr")], writes=[PSB[out_bank]])

        def norm(ip, gcol, dst_is_h=True):
            pa = PASSES[ip]
            for si, (c0, n) in enumerate(pa["subs"]):
                blks = range(c0 // 128, (c0 + n) // 128)
                for k in range(KD):
                    pg.op(act, lambda e, k=k, c0=c0, n=n: e.activation(out=WB[:, k, c0:c0 + n], in_=xT[:, k, c0:c0 + n], func=AF.Square),
                          reads=[B("xT", k, b) for b in blks], writes=[B("WB", k, si)])
                rstd_bcast([WB[:, k, c0:c0 + n] for k in range(KD)], [B("WB", k, si) for k in range(KD)], n, 1.0 / D, 6, 7)
                for k in range(KD):
                    pg.op(dve, lambda e, k=k, c0=c0, n=n: e.scalar_tensor_tensor(out=hT[:, k, c0:c0 + n], in0=xT[:, k, c0:c0 + n],
                                                                                 scalar=pcol(gcol + k), in1=ps[:, 7, 0:n],
                                                                                 op0=ALU.mult, op1=ALU.mult),
                          reads=[B("xT", k, b) for b in blks] + [PSB[7], B("par")], writes=[B("hT", si)])

        def xupdate(ip, m, si, c0, n, bank, scale):
            blks = range(c0 // 128, (c0 + n) // 128)
            xb = [B("xT", m, b) for b in blks]
            pg.op(dve, lambda e: e.scalar_tensor_tensor(out=xT[:, m, c0:c0 + n], in0=ps[:, bank, 0:n], scalar=scale,
                                                        in1=xT[:, m, c0:c0 + n], op0=ALU.mult, op1=ALU.add),
                  reads=[PSB[bank]] + xb, writes=xb)

        def ffn(ip, tag):
            pa = PASSES[ip]
            it = 0
            ity = 0
            for g, (j0, nj) in enumerate(FFN_GROUPS):
                for jj in range(nj):
                    j = j0 + jj
                    w1v, w1b = W.get((tag, "w1", j))
                    w3v, w3b = W.get((tag, "w3", j))
                    for si, (c0, n) in enumerate(pa["subs"]):
                        b1, b3 = it % 2, 2 + it % 2
                        st = tmp[it % 2]
                        it += 1
                        pg.group(pe, [lambda e, k=k, w1v=w1v, b1=b1, c0=c0, n=n: e.matmul(ps[:, b1, 0:n], lhsT=w1v[:, k, :], rhs=hT[:, k, c0:c0 + n],
                                                                                     start=(k == 0), stop=(k == KD - 1)) for k in range(KD)],
                                 reads=[w1b, B("hT", si)], writes=[PSB[b1]])
                        pg.group(pe, [lambda e, k=k, w3v=w3v, b3=b3, c0=c0, n=n: e.matmul(ps[:, b3, 0:n], lhsT=w3v[:, k, :], rhs=hT[:, k, c0:c0 + n],
                                                                                     start=(k == 0), stop=(k == KD - 1)) for k in range(KD)],
                                 reads=[w3b, B("hT", si)], writes=[PSB[b3]])
                        pg.op(act, lambda e, st=st, b1=b1, n=n: e.activation(out=st[:, 0:n], in_=ps[:, b1, 0:n], func=AF.Silu),
                              reads=[PSB[b1]], writes=[TB[(it - 1) % 2]])
                        pg.op(dve, lambda e, st=st, b3=b3, jj=jj, c0=c0, n=n: e.tensor_tensor(out=WB[:, jj, c0:c0 + n], in0=st[:, 0:n],
                                                                                             in1=ps[:, b3, 0:n], op=ALU.mult),
                              reads=[TB[(it - 1) % 2], PSB[b3]], writes=[B("WB", jj, si)])
                    W.release(2)
                for m in range(8):
                    w2v, w2b = W.get((tag, "w2", g, m))
                    for si, (c0, n) in enumerate(pa["subs"]):
                        by = 4 + ity % 4
                        ity += 1
                        pg.group(pe, [lambda e, jj=jj, w2v=w2v, by=by, c0=c0, n=n, nj=nj: e.matmul(ps[:, by, 0:n], lhsT=w2v[:, jj, :], rhs=WB[:, jj, c0:c0 + n],
                                                                                            start=(jj == 0), stop=(jj == nj - 1)) for jj in range(nj)],
                                 reads=[w2b] + [B("WB", jj, si) for jj in range(nj)], writes=[PSB[by]])
                        xupdate(ip, m, si, c0, n, by, 0.5)
                    W.release(1)

        def final(ip):
            pa = PASSES[ip]
            for si, (c0, n) in enumerate(pa["subs"]):
                blks = range(c0 // 128, (c0 + n) // 128)
                for k in range(KD):
                    pg.op(act, lambda e, k=k, c0=c0, n=n: e.activation(out=WB[:, k, c0:c0 + n], in_=xT[:, k, c0:c0 + n], func=AF.Square),
                          reads=[B("xT", k, b) for b in blks], writes=[B("WB", k, si)])
                rstd_bcast([WB[:, k, c0:c0 + n] for k in range(KD)], [B("WB", k, si) for k in range(KD)], n, 1.0 / D, 5, 4)
                for k in range(KD):
                    pg.op(dve, lambda e, k=k, c0=c0, n=n: e.scalar_tensor_tensor(out=tmp[k][:, 0:n], in0=xT[:, k, c0:c0 + n],
                                                                                 scalar=pcol(PGF + k), in1=ps[:, 4, 0:n],
                                                                                 op0=ALU.mult, op1=ALU.mult),
                          reads=[B("xT", k, b) for b in blks] + [PSB[4], B("par")], writes=[TB[k]])
                for b in blks:
                    bc = b * 128
                    st = xst[b % 2]
                    STB = [B("xst", b % 2, 0), B("xst", b % 2, 1)]
                    for hf in range(2):
                        bank = [6, 7, 2, 3][(b % 2) * 2 + hf]
                        pg.group(pe, [lambda e, k=k, bank=bank, lc=bc - c0: e.transpose(ps[:, bank, (k % 4) * 128:(k % 4 + 1) * 128],
                                                                                         tmp[k][:, lc:lc + 128], ident)
                                      for k in range(hf * 4, hf * 4 + 4)], reads=[TB[k] for k in range(hf * 4, hf * 4 + 4)] + [CST], writes=[PSB[bank]])
                        if hf == 0:
                            pg.op(act, lambda e, st=st, bank=bank: e.activation(out=st[:, 0:512], in_=ps[:, bank, :], func=AF.Copy),
                                  reads=[PSB[bank]], writes=[STB[0]])
                        else:
                            pg.op(dve, lambda e, st=st, bank=bank: e.tensor_copy(st[:, 512:1024], ps[:, bank, :]),
                                  reads=[PSB[bank]], writes=[STB[1]])
                    if b < pa["npb"]:
                        r0 = pa["ptok0"] + b * 128
                        dst = yp_d[r0:r0 + 128, :]
                    else:
                        dst = ys_d[:, :]
                    pg.dma(sp, ysem[b % 2], lambda e, st=st, dst=dst: e.dma_start(out=dst, in_=st[:]), reads=STB)

        def mixer(ip):
            pa = PASSES[ip]
            T, subs, npb, sample = pa["T"], pa["subs"], pa["npb"], pa["sample"]
            nblk = T // 128
            last_pass = (ip == len(PASSES) - 1)
            norm(ip, PGM)

            def sub_of(b):
                for si_, (c0_, n_) in enumerate(subs):
                    if c0_ <= b * 128 < c0_ + n_:
                        return si_

            wv = [W.get(("wv", kp)) for kp in range(4)]
            for b in range(nblk):
                bank = b % 2
                pg.group(pe, [mm(ps[:, bank, :], hT[:, k, b * 128:(b + 1) * 128], wv[k // 2][0][:, k % 2, :], k == 0, k == KD - 1)
                              for k in range(KD)],
                         reads=[B("hT", sub_of(b))] + [x[1] for x in wv], writes=[PSB[bank]])
                if b % 2 == 0:
                    pg.op(act, lambda e, b=b, bank=bank: e.activation(out=vtok[:, b, :], in_=ps[:, bank, :], func=AF.Copy),
                          reads=[PSB[bank]], writes=[B("vtok", b)])
                else:
                    pg.op(dve, lambda e, b=b, bank=bank: e.tensor_copy(vtok[:, b, :], ps[:, bank, :]),
                          reads=[PSB[bank]], writes=[B("vtok", b)])
            W.release(4)

            if stage < 2.2:
                return
            psb1 = ps[:, 1, :].bitcast(BF16)
            NRING = 24

            def SH(e_):
                e_ = e_ % NRING
                return WB[:, 5 + e_ // 8, (e_ % 8) * 128:(e_ % 8 + 1) * 128]

            st = dict(bi=0, ring=0, ap=0, a3=0)
            items = [(h, si) for h in range(4) for si in range(len(subs))]
            wts = {}
            info = {}

            def stage_A(h, si):
                c0, n = subs[si]
                if si == 0:
                    wts[h] = (W.get(("wq", h)), W.get(("wf", h)), W.get(("wog", h)))
                    e0 = st["ring"]
                    st["ring"] += 1
                    pg.op(dve, lambda e, h=h, e0=e0: e.tensor_copy(SH(e0), S[:, h, :]), reads=[B("S", h), B("S")], writes=[B("SH", e0 % NRING)])
                    info[("cur", h)] = e0
                (wq, wqb), (wf, wfb), (wg, wgb) = wts[h]
                ap = st["ap"] % 2
                st["ap"] += 1
                has_s = sample and (c0 + n > npb * 128)
                npc = (npb * 128 - c0) if has_s else n
                Pt, PB_ = (tmp[6], TB[6]) if ap == 0 else (tmp[7], TB[7])
                a3 = st["a3"] % 3
                st["a3"] += 1
                sg, SGB = [(tmp[1], TB[1]), (tmp[2], TB[2]), (tmp[3], TB[3])][a3]
                kd_, KDB = kdb[ap], B("kdb", ap)
                qd_, QDB = qdb[ap], B("qdb", ap)
                kk_, KKB = kkb[ap], B("kkb", ap)
                qr_, QRB = WB[:, [8, 9, 4][a3], 0:512], B("qdR", a3)
                info[(h, si)] = dict(ap=ap, Pt=Pt, PB_=PB_, sg=sg, SGB=SGB, kd_=kd_, KDB=KDB, qd_=qd_, QDB=QDB, kk_=kk_, KKB=KKB,
                                     qr_=qr_, QRB=QRB, has_s=has_s, npc=npc, obank=3 + ap)
                proj(wf, wfb, 1, si, c0, n)
                pg.op(act, lambda e: e.activation(out=tmp[0][:, 0:n], in_=ps[:, 1, 0:n], func=AF.Tanh, scale=0.5),
                      reads=[PSB[1]], writes=[TB[0]])
                pg.op(act, lambda e: e.activation(out=tmp[4][:, 0:n], in_=tmp[0][:, 0:n], func=AF.Identity, scale=drv[:, 4 + h:5 + h],
                                                  bias=drv[:, h:h + 1]), reads=[TB[0], DRV], writes=[TB[4]])
                pg.op(act, lambda e: e.activation(out=tmp[5][:, 0:n], in_=tmp[0][:, 0:n], func=AF.Identity, scale=drv[:, h:h + 1],
                                                  bias=drv[:, 8 + h:9 + h]), reads=[TB[0], DRV], writes=[TB[5]])
                yield
                proj(wq, wqb, 0, si, c0, n)
                yield
                proj(wg, wgb, 2, si, c0, n)
                if si == len(subs) - 1:
                    W.release(3)
                pg.op(act, lambda e: e.activation(out=sg[:, 0:n], in_=ps[:, 2, 0:n], func=AF.Silu),
                      reads=[PSB[2]], writes=[SGB])
                yield
                rm0 = C_RMIX if has_s else C_R64
                pg.op(dve, lambda e: e.tensor_tensor_scan(out=Pt[:, 0:n], data0=cst[:, rm0:rm0 + n], data1=tmp[5][:, 0:n],
                                                          initial=1.0, op0=ALU.max, op1=ALU.mult),
                      reads=[TB[5], CST], writes=[PB_])
                yield
                if False:
                    pg.op(pool, lambda e: e.tensor_tensor(out=kd_[:, 0:n], in0=tmp[4][:, 0:n], in1=Pt[:, 0:n], op=ALU.divide),
                          reads=[TB[4], PB_], writes=[KDB])
                else:
                    pg.op(dve, lambda e: e.reciprocal(tmp[5][:, 0:n], Pt[:, 0:n]), reads=[PB_], writes=[TB[5]])
                    pg.op(dve, lambda e: e.tensor_tensor(out=kd_[:, 0:n], in0=tmp[4][:, 0:n], in1=tmp[5][:, 0:n], op=ALU.mult),
                          reads=[TB[4], TB[5]], writes=[KDB])
                yield
                pg.op(dve, lambda e: e.tensor_tensor(out=qr_[:, 0:n], in0=ps[:, 0, 0:n], in1=Pt[:, 0:n], op=ALU.mult),
                      reads=[PSB[0], PB_], writes=[QRB])
                pg.op(dve, lambda e: e.tensor_copy(qd_[:, 0:n], qr_[:, 0:n].bitcast(F32)), reads=[QRB], writes=[QDB])
                yield
                nch = npc // 64
                pg.op(dve, lambda e: e.tensor_tensor(out=kk_[:, 0:npc].rearrange("p (c t) -> p c t", t=64),
                                                      in0=kd_[:, 0:npc].rearrange("p (c t) -> p c t", t=64),
                                                      in1=Pt[:, 0:npc].rearrange("p (c t) -> p c t", t=64)[:, :, 63:64].broadcast_to([128, nch, 64]),
                                                      op=ALU.mult), reads=[KDB, PB_], writes=[KKB])
                if has_s:
                    pg.op(dve, lambda e: e.tensor_tensor(out=kk_[:, npc:n].rearrange("p (c t) -> p c t", t=8),
                                                          in0=kd_[:, npc:n].rearrange("p (c t) -> p c t", t=8),
                                                          in1=Pt[:, npc:n].rearrange("p (c t) -> p c t", t=8)[:, :, 7:8].broadcast_to([128, 16, 8]),
                                                          op=ALU.mult), reads=[KDB, PB_], writes=[KKB])

            def stage_B1(h, si):
                c0, n = subs[si]
                I = info[(h, si)]
                Pt, PB_, kd_, KDB, qd_, QDB, kk_, KKB, qr_, QRB, obank = (I["Pt"], I["PB_"], I["kd_"], I["KDB"], I["qd_"], I["QDB"],
                                                                           I["kk_"], I["KKB"], I["qr_"], I["QRB"], I["obank"])
                I["chunks"] = []
                blks = list(range(c0 // 128, (c0 + n) // 128))
                nb = len(blks)
                has_s = I["has_s"]
                npb_ = nb - 1 if has_s else nb
                st["bi"] += 1
                j = st["bi"] % 2
                kt, KTB = kkT[j], B("kkT", j)
                sm, SMB = scmb[j], B("scmb", j)
                b0 = blks[0]
                hc = slice(h * 128, (h + 1) * 128)
                if has_s:
                    src0 = sh_d[0:4, h].rearrange("s k v -> k s v")
                    pg.dma(sp, s0sem2[0], lambda e, src0=src0: e.dma_start(out=s0t[:, 0:4, :], in_=src0), writes=[B("s0t", 0)])
                pg.op(pool, lambda e: e.tensor_tensor(out=vblk[:, 0:2 * npb_, :].rearrange("p (b c) v -> p b c v", c=2),
                                                      in0=vtok[:, b0:b0 + npb_, hc].unsqueeze(2).broadcast_to([128, npb_, 2, 128]),
                                                      in1=cm2b[:, 0:2].unsqueeze(1).unsqueeze(3).broadcast_to([128, npb_, 2, 128]), op=ALU.mult),
                      reads=[B("vtok", b) for b in blks[:npb_]] + [B("cm2b")], writes=[B("vblk")])
                yield
                pg.group(pe, [lambda e, ib=ib: e.transpose(psb1[:, ib * 128:(ib + 1) * 128], kk_[:, ib * 128:(ib + 1) * 128], identb[:]) for ib in range(nb)],
                         reads=[KKB, B("identb")], writes=[PSB[1]])
                pg.op(act, lambda e: e.activation(out=kt[:, 0:nb * 128], in_=psb1[:, 0:nb * 128], func=AF.Copy), reads=[PSB[1]], writes=[KTB])
                yield
                pg.group(pe, [mm(ps[:, 5, ib * 128:(ib + 1) * 128], kd_[:, ib * 128:(ib + 1) * 128], qd_[:, ib * 128:(ib + 1) * 128], True, True, True)
                              for ib in range(nb)], reads=[KDB, QDB], writes=[PSB[5]])
                mt = maskMix if has_s else maskP4
                pg.op(dve, lambda e: e.tensor_tensor(out=sm[:, 0:nb * 128], in0=ps[:, 5, 0:nb * 128], in1=mt[:, 0:nb * 128], op=ALU.mult),
                      reads=[PSB[5], B("masks")], writes=[SMB])
                yield
                dbufs = [PSB[6]] + ([PSB[7]] if npb_ > 2 else [])
                pg.group(pe, [mm(ps[:, 6 + ib // 2, (ib % 2) * 256:(ib % 2) * 256 + 256], kt[:, ib * 128:(ib + 1) * 128], vblk[:, 2 * ib:2 * ib + 2, :], True, True, True)
                              for ib in range(npb_)], reads=[KTB, B("vblk")], writes=dbufs)
                pg.group(pe, [mm(ps[:, obank, ib * 128:(ib + 1) * 128], vtok[:, blks[ib], hc], sm[:, ib * 128:(ib + 1) * 128], ib == 0, False, True)
                              for ib in range(nb)], reads=[B("vtok", b) for b in blks] + [SMB], writes=[PSB[obank]])
                yield
                for ib in range(npb_):
                    lc = ib * 128
                    dbank = 6 + ib // 2
                    dcol = (ib % 2) * 256
                    for c2 in range(2):
                        cc = lc + c2 * 64
                        eprev = info[("cur", h)]
                        enew = st["ring"]
                        st["ring"] += 1
                        pg.op(dve, lambda e, eprev=eprev, enew=enew, cc=cc, dbank=dbank, dcol=dcol, c2=c2: e.scalar_tensor_tensor(
                            out=SH(enew), in0=SH(eprev).bitcast(F32), scalar=Pt[:, cc + 63:cc + 64],
                            in1=ps[:, dbank, dcol + c2 * 128:dcol + (c2 + 1) * 128], op0=ALU.mult, op1=ALU.add),
                            reads=[B("SH", eprev % NRING), PB_, PSB[dbank]], writes=[B("SH", enew % NRING)])
                        I["chunks"].append((cc, eprev, c2 == 1))
                        info[("cur", h)] = enew
                    yield
                if has_s:
                    ib = nb - 1
                    lc = ib * 128
                    b = blks[ib]
                    vsl = vtok[:, b, hc]
                    ktS = kt[:, ib * 128:(ib + 1) * 128]
                    for r4 in range(4):
                        jb = r4 % 2
                        if r4 + 1 < 4:
                            srcn = sh_d[(r4 + 1) * 4:(r4 + 2) * 4, h].rearrange("s k v -> k s v")
                            jn = (r4 + 1) % 2
                            pg.dma(sp, s0sem2[jn], lambda e, srcn=srcn, jn=jn: e.dma_start(out=s0t[:, jn * 4:jn * 4 + 4, :], in_=srcn),
                                   writes=[B("s0t", jn)])
                        pg.group(pe, [mm(ps[:, obank, lc + (r4 * 4 + i) * 8:lc + (r4 * 4 + i) * 8 + 8], s0t[:, jb * 4 + i, :],
                                         qr_[:, lc + (r4 * 4 + i) * 8:lc + (r4 * 4 + i) * 8 + 8], False, (r4 == 3 and i == 3), True) for i in range(4)],
                                 reads=[B("s0t", jb), QRB], writes=[PSB[obank]])
                        pg.op(pool, lambda e, r4=r4: e.tensor_tensor(
                            out=vblk[:, 4:8, :], in0=vsl.unsqueeze(1).broadcast_to([128, 4, 128]),
                            in1=m2b[:, r4 * 4:r4 * 4 + 4].unsqueeze(2).broadcast_to([128, 4, 128]), op=ALU.mult),
                            reads=[B("vtok", b), B("m2b")], writes=[B("vblk")])
                        pg.group(pe, [mm(ps[:, 7, :], ktS, vblk[:, 4:8, :], True, True, True)], reads=[KTB, B("vblk")], writes=[PSB[7]])
                        for i4 in range(4):
                            pl = lc + (r4 * 4 + i4) * 8 + 7
                            pg.op(dve, lambda e, jb=jb, i4=i4, pl=pl: e.scalar_tensor_tensor(
                                out=snw[:, jb * 4 + i4, :], in0=s0t[:, jb * 4 + i4, :].bitcast(F32), scalar=Pt[:, pl:pl + 1],
                                in1=ps[:, 7, i4 * 128:(i4 + 1) * 128], op0=ALU.mult, op1=ALU.add),
                                reads=[B("s0t", jb), PB_, PSB[7]], writes=[B("snw", jb)])
                        dst = shs_d[r4 * 4:(r4 + 1) * 4, h].rearrange("s k v -> k s v")
                        pg.dma(sp, snsem2[jb], lambda e, dst=dst, jb=jb: e.dma_start(out=dst, in_=snw[:, jb * 4:jb * 4 + 4, :]), reads=[B("snw", jb)])
                        yield
                if si == len(subs) - 1:
                    ecur = info[("cur", h)]
                    pg.op(dve, lambda e, ecur=ecur: e.tensor_copy(S[:, h, :], SH(ecur).bitcast(F32)), reads=[B("SH", ecur % NRING)], writes=[B("S", h)])

            def stage_B2G(h, si):
                c0, n = subs[si]
                I = info[(h, si)]
                obank, qr_, QRB, sg, SGB = I["obank"], I["qr_"], I["QRB"], I["sg"], I["SGB"]
                for (cc, eprev, last) in I["chunks"]:
                    pg.group(pe, [mm(ps[:, obank, cc:cc + 64], SH(eprev), qr_[:, cc:cc + 64], False, last, True)],
                             reads=[B("SH", eprev % NRING), QRB], writes=[PSB[obank]])
                yield
                pg.op(act, lambda e: e.activation(out=osq[:, 0:n], in_=ps[:, obank, 0:n], func=AF.Square), reads=[PSB[obank]], writes=[B("osq")])
                yield
                rstd_bcast([osq[:, 0:n]], [B("osq")], n, 1.0 / 128, 2, 2, split=True)
                yield
                yield
                rstd_b(n, 2)
                pg.op(dve, lambda e: e.tensor_tensor(out=sg[:, 0:n], in0=sg[:, 0:n], in1=ps[:, 2, 0:n], op=ALU.mult),
                      reads=[SGB, PSB[2]], writes=[SGB])
                pg.op(dve, lambda e: e.scalar_tensor_tensor(out=WB[:, h, c0:c0 + n], in0=ps[:, obank, 0:n], scalar=pcol(PGH + h),
                                                            in1=sg[:, 0:n], op0=ALU.mult, op1=ALU.mult),
                      reads=[PSB[obank], SGB, B("par")], writes=[B("WB", h, si)])

            def drive(gens):
                gens = [g for g in gens if g is not None]
                while gens:
                    for g in list(gens):
                        try:
                            next(g)
                        except StopIteration:
                            gens.remove(g)

            drive([stage_A(*items[0])])
            for idx in range(len(items)):
                gB2G = stage_B2G(*items[idx - 1]) if idx >= 1 else iter(())
                gB1 = stage_B1(*items[idx])
                gA = stage_A(*items[idx + 1]) if idx + 1 < len(items) else iter(())

                def step(g_, k_=1):
                    for _ in range(k_):
                        try:
                            next(g_)
                        except StopIteration:
                            return
                for g_ in (gA, gB2G, gB1, gB1, gB2G, gA, gB1, gB2G, gA, gA, gA, gB1):
                    step(g_)
                c0_, n_ = subs[items[idx][1]]
                has_s_ = sample and (c0_ + n_ > npb * 128)
                step(gB1, n_ // 128 - (1 if has_s_ else 0))
                drive([gA])
                drive([gB1])
                drive([gB2G])
            drive([stage_B2G(*items[-1])])
            if last_pass:
                pg.dma(sp, osem, lambda e: e.dma_start(out=shp_d.rearrange("h k v -> k h v"), in_=S[:]),
                       reads=[B("S", h) for h in range(4)])

            if stage < 2.3:
                return
            itc = 0
            for c in range(4):
                wbg, wbgb = W.get(("wbg", c))
                wcg, wcgb = W.get(("wcg", c))
                wvv, wvvb = W.get(("wvv", c))
                w0, w1, w2 = pcol(PCW + c * 3 + 0), pcol(PCW + c * 3 + 1), pcol(PCW + c * 3 + 2)
                for si, (c0, n) in enumerate(subs):
                    itc += 1
                    bA, bB, bC = itc % 2, 2 + itc % 2, 4 + itc % 2
                    proj(wbg, wbgb, bA, si, c0, n)
                    proj(wcg, wcgb, bB, si, c0, n)
                    proj(wvv, wvvb, bC, si, c0, n)
                    npc = (min(c0 + n, npb * 128) - c0) if sample else n
                    has_s = n > npc
                    vvs, VVB = tmp[itc % 2], TB[itc % 2]
                    ut, UTB = tmp[4 + itc % 2], TB[4 + itc % 2]
                    pg.op(act, lambda e, vvs=vvs, bC=bC, n=n: e.activation(out=vvs[:, 0:n], in_=ps[:, bC, 0:n], func=AF.Copy),
                          reads=[PSB[bC]], writes=[VVB])
                    pg.op(pool, lambda e, ut=ut, c=c: e.tensor_copy(ut[:, 0:2], ucar[:, c, :]), reads=[B("ucar", c), B("ucar")], writes=[UTB])
                    pg.op(dve, lambda e, ut=ut, vvs=vvs, bB=bB, npc=npc: e.tensor_tensor(out=ut[:, 2:2 + npc], in0=ps[:, bB, 0:npc], in1=vvs[:, 0:npc], op=ALU.mult),
                          reads=[PSB[bB], VVB], writes=[UTB])
                    pg.op(pool, lambda e, ut=ut, c=c, npc=npc: e.tensor_copy(ucar[:, c, :], ut[:, npc:npc + 2]), reads=[UTB], writes=[B("ucar", c)])
                    pg.op(dve, lambda e, ut=ut, npc=npc, w2=w2: e.tensor_scalar(out=tmp[6][:, 0:npc], in0=ut[:, 2:2 + npc], scalar1=w2, scalar2=None, op0=ALU.mult),
                          reads=[UTB, B("par")], writes=[TB[6]])
                    pg.op(dve, lambda e, ut=ut, npc=npc, w1=w1: e.scalar_tensor_tensor(out=tmp[6][:, 0:npc], in0=ut[:, 1:1 + npc], scalar=w1, in1=tmp[6][:, 0:npc],
                                                                                      op0=ALU.mult, op1=ALU.add), reads=[UTB, TB[6]], writes=[TB[6]])
                    pg.op(dve, lambda e, ut=ut, npc=npc, w0=w0: e.scalar_tensor_tensor(out=tmp[6][:, 0:npc], in0=ut[:, 0:npc], scalar=w0, in1=tmp[6][:, 0:npc],
                                                                                      op0=ALU.mult, op1=ALU.add), reads=[UTB, TB[6]], writes=[TB[6]])
                    pg.op(dve, lambda e, bA=bA, npc=npc, c=c, c0=c0: e.tensor_tensor(out=WB[:, 4 + c, c0:c0 + npc], in0=ps[:, bA, 0:npc], in1=tmp[6][:, 0:npc], op=ALU.mult),
                          reads=[PSB[bA], TB[6]], writes=[B("WB", 4 + c, si)])
                    if has_s:
                        def v3(ap):
                            return ap.rearrange("p (i t) -> p i t", t=8)
                        us = usmp[:, c]
                        USB = B("usmp", c)
                        cvs = v3(tmp[7][:, 0:128])
                        pg.op(dve, lambda e, us=us, vvs=vvs, bB=bB, npc=npc, n=n: e.tensor_tensor(out=us[:, :, 2:10], in0=v3(ps[:, bB, npc:n]), in1=v3(vvs[:, npc:n]), op=ALU.mult),
                              reads=[PSB[bB], VVB, USB], writes=[USB])
                        pg.op(dve, lambda e, us=us, cvs=cvs, w2=w2: e.tensor_scalar(out=cvs, in0=us[:, :, 2:10], scalar1=w2, scalar2=None, op0=ALU.mult),
                              reads=[USB, B("par")], writes=[TB[7]])
                        pg.op(dve, lambda e, us=us, cvs=cvs, w1=w1: e.scalar_tensor_tensor(out=cvs, in0=us[:, :, 1:9], scalar=w1, in1=cvs, op0=ALU.mult, op1=ALU.add),
                              reads=[USB, TB[7]], writes=[TB[7]])
                        pg.op(dve, lambda e, us=us, cvs=cvs, w0=w0: e.scalar_tensor_tensor(out=cvs, in0=us[:, :, 0:8], scalar=w0, in1=cvs, op0=ALU.mult, op1=ALU.add),
                              reads=[USB, TB[7]], writes=[TB[7]])
                        pg.op(dve, lambda e, bA=bA, npc=npc, n=n, c=c, c0=c0: e.tensor_tensor(out=WB[:, 4 + c, c0 + npc:c0 + n], in0=ps[:, bA, npc:n], in1=tmp[7][:, 0:128], op=ALU.mult),
                              reads=[PSB[bA], TB[7]], writes=[B("WB", 4 + c, si)])
                W.release(3)
            if sample:
                for c in range(4):
                    pg.op(dve, lambda e, c=c: e.tensor_copy(tmp[6][:, c * 32:(c + 1) * 32].rearrange("p (i t) -> p i t", t=2), usmp[:, c, :, 8:10]),
                          reads=[B("usmp", c)], writes=[TB[6]])
                pg.group(pe, [lambda e, c=c: e.transpose(ps[0:32, 7, c * 128:(c + 1) * 128], tmp[6][:, c * 32:(c + 1) * 32], ident) for c in range(4)],
                         reads=[TB[6], CST], writes=[PSB[7]])
                pg.op(dve, lambda e: e.tensor_copy(sco, ps[0:32, 7, :]), reads=[PSB[7]], writes=[TB[7]])
                pg.dma(sp, osem, lambda e: e.dma_start(out=scs_d, in_=sco), reads=[TB[7]])
            if last_pass:
                pg.group(pe, [lambda e, c=c: e.transpose(ps[0:2, 7, c * 128:(c + 1) * 128], ucar[:, c, :], ident) for c in range(4)],
                         reads=[B("ucar", c) for c in range(4)] + [CST], writes=[PSB[7]])
                pg.op(dve, lambda e: e.tensor_copy(tmp[7][0:2, 0:512], ps[0:2, 7, :]), reads=[PSB[7]], writes=[TB[7]])
                pg.dma(sp, osem, lambda e: e.dma_start(out=scp_d, in_=tmp[7][0:2, 0:512]), reads=[TB[7]])

            if stage < 2.4:
                return
            it4 = 0
            ity = 0
            for qt in range(4):
                for mm_ in range(2):
                    m = qt * 2 + mm_
                    wga, wgab = W.get(("wga", m))
                    wgb, wgbb = W.get(("wgb", m))
                    wab, wabb = W.get(("wab", m))
                    wao, waob = wab[:, 0:4, :], wabb
                    wbo, wbob = wab[:, 4:8, :], wabb
                    for si, (c0, n) in enumerate(subs):
                        it4 += 1
                        st_ = it4 % 2
                        bk = [0, 1, 2, 3] if st_ == 0 else [4, 5, 6, 7]
                        proj(wga, wgab, bk[0], si, c0, n)
                        proj(wgb, wgbb, bk[1], si, c0, n)
                        pg.group(pe, [mm(ps[:, bk[2], 0:n], wao[:, kk, :], WB[:, kk, c0:c0 + n], kk == 0, kk == 3) for kk in range(4)],
                                 reads=[waob] + [B("WB", kk, si) for kk in range(4)], writes=[PSB[bk[2]]])
                        pg.group(pe, [mm(ps[:, bk[3], 0:n], wbo[:, kk, :], WB[:, 4 + kk, c0:c0 + n], kk == 0, kk == 3) for kk in range(4)],
                                 reads=[wbob] + [B("WB", 4 + kk, si) for kk in range(4)], writes=[PSB[bk[3]]])
                        ta, TAB = tmp[st_], TB[st_]
                        tb, TBB = tmp[2 + st_], TB[2 + st_]
                        t1, T1B = tmp[4 + st_], TB[4 + st_]
                        t2, T2B = tmp[6 + st_], TB[6 + st_]
                        pg.op(act, lambda e, ta=ta, b0=bk[0], n=n: e.activation(out=ta[:, 0:n], in_=ps[:, b0, 0:n], func=AF.Tanh, scale=0.5),
                              reads=[PSB[bk[0]]], writes=[TAB])
                        pg.op(act, lambda e, tb=tb, b1=bk[1], n=n: e.activation(out=tb[:, 0:n], in_=ps[:, b1, 0:n], func=AF.Tanh, scale=0.5),
                              reads=[PSB[bk[1]]], writes=[TBB])
                        pg.op(dve, lambda e, ta=ta, t1=t1, b2=bk[2], n=n: e.scalar_tensor_tensor(out=t1[:, 0:n], in0=ta[:, 0:n], scalar=1.0, in1=ps[:, b2, 0:n],
                                                                                              op0=ALU.add, op1=ALU.mult),
                              reads=[TAB, PSB[bk[2]]], writes=[T1B])
                        pg.op(dve, lambda e, tb=tb, t2=t2, b3=bk[3], n=n: e.scalar_tensor_tensor(out=t2[:, 0:n], in0=tb[:, 0:n], scalar=1.0, in1=ps[:, b3, 0:n],
                                                                                              op0=ALU.add, op1=ALU.mult),
                              reads=[TBB, PSB[bk[3]]], writes=[T2B])
                        pg.op(pool, lambda e, t1=t1, t2=t2, mm_=mm_, c0=c0, n=n: e.tensor_tensor(out=WB[:, 8 + mm_, c0:c0 + n], in0=t1[:, 0:n], in1=t2[:, 0:n], op=ALU.add),
                              reads=[T1B, T2B], writes=[B("WB", 8 + mm_, si)])
                    W.release(3)
                for mo in range(8):
                    wo_, wob = W.get(("wo", qt, mo))
                    for si, (c0, n) in enumerate(subs):
                        ity += 1
                        by = [0, 4, 1, 5][ity % 4]
                        pg.group(pe, [mm(ps[:, by, 0:n], wo_[:, kk, :], WB[:, 8 + kk, c0:c0 + n], kk == 0, kk == 1) for kk in range(2)],
                                 reads=[wob] + [B("WB", 8 + kk, si) for kk in range(2)], writes=[PSB[by]])
                        xupdate(ip, mo, si, c0, n, by, 0.5)
                    W.release(1)

        if PASSES[0]["sample"]:
            load_sample_conv_state()
        for ip in range(len(PASSES)):
            load_x(ip)
            if ip == 0:
                W = WRing(pg, arena, wsems, plan)
            if stage >= 1:
                norm(ip, PG1)
                ffn(ip, "a")
            if stage >= 2:
                mixer(ip)
            if stage >= 3:
                norm(ip, PG2)
                ffn(ip, "b")
            final(ip)

        for ds_ in ysem + [osem, snsem] + snsem2:
            if ds_.count:
                sp.wait(Tok(ds_.sem, ds_.count, None))
        pg.replay(block)
    return nc


def _consts():
    c = np.zeros((128, NCST), np.float32)
    idx = np.arange(128)
    c[:, C_ID:C_ID + 128] = np.eye(128, dtype=np.float32)
    c[:, C_ONE:C_ONE + 128] = 1.0
    s, t = idx[:, None], idx[None, :]
    c[:, C_MP:C_MP + 128] = ((s // 64 == t // 64) & (s <= t)).astype(np.float32)
    c[:, C_MS:C_MS + 128] = ((s // 8 == t // 8) & (s <= t)).astype(np.float32)
    r64 = (np.arange(512) % 64 == 0).astype(np.float32)
    c[:, C_R64:C_R64 + 512] = r64[None, :]
    rmix = np.concatenate([(np.arange(256) % 64 == 0), (np.arange(128) % 8 == 0)]).astype(np.float32)
    c[:, C_RMIX:C_RMIX + 384] = rmix[None, :]
    c[:, C_M2:C_M2 + 16] = (idx[:, None] // 8 == np.arange(16)[None, :]).astype(np.float32)
    c[:, C_CM2:C_CM2 + 2] = (idx[:, None] // 64 == np.arange(2)[None, :]).astype(np.float32)
    return c


def _fm(v, nchunk):
    return np.ascontiguousarray(np.asarray(v, np.float32).reshape(nchunk, 128).T)


_PROG_CACHE = {}


def kernel(x_prompt, x_sample, state_hgrn, state_conv, lower_bound_logits, g_ffn1, w1_ffn1, w3_ffn1, w2_ffn1,
           g_mix, w_in, conv_w, g_hgrn_out, w_a_out, w_b_out, w_o, g_ffn2, w1_ffn2, w3_ffn2, w2_ffn2, g_final,
           _stage=99):
    f32 = np.float32
    par = np.zeros((128, NPAR), f32)
    par[:, PG1:PG1 + 8] = _fm(g_ffn1[0], 8)
    par[:, PGM:PGM + 8] = _fm(g_mix[0], 8)
    par[:, PG2:PG2 + 8] = _fm(g_ffn2[0], 8)
    par[:, PGF:PGF + 8] = _fm(g_final, 8)
    par[:, PL0:PL0 + 4] = _fm(lower_bound_logits[0], 4)
    par[:, PL1:PL1 + 4] = _fm(lower_bound_logits[1], 4)
    par[:, PGH:PGH + 4] = _fm(g_hgrn_out[0], 4)
    for c in range(4):
        for j in range(3):
            par[:, PCW + c * 3 + j] = np.asarray(conv_w[0, j, c * 128:(c + 1) * 128], f32)
    cst = _consts()
    shared = dict(par=par, cst=cst,
                  w1a=np.ascontiguousarray(w1_ffn1[0], f32), w3a=np.ascontiguousarray(w3_ffn1[0], f32),
                  w2a=np.ascontiguousarray(w2_ffn1[0], f32), win=np.ascontiguousarray(w_in[0], f32),
                  wab=np.ascontiguousarray(np.concatenate([w_a_out[0], w_b_out[0]], axis=0), f32),
                  wo=np.ascontiguousarray(w_o[0], f32),
                  w1b=np.ascontiguousarray(w1_ffn2[0], f32), w3b=np.ascontiguousarray(w3_ffn2[0], f32),
                  w2b=np.ascontiguousarray(w2_ffn2[0], f32))
    in_maps = []
    for c in range(NCORES):
        m = dict(shared)
        m["xp"] = np.ascontiguousarray(x_prompt[c], f32)
        m["xs"] = np.ascontiguousarray(x_sample[c * 16:(c + 1) * 16], f32).reshape(128, D)
        m["sh"] = np.ascontiguousarray(state_hgrn[0, c * 16:(c + 1) * 16], f32)
        m["sc"] = np.ascontiguousarray(state_conv[0, c * 16:(c + 1) * 16], f32).reshape(32, 512)
        in_maps.append(m)
    if _stage not in _PROG_CACHE:
        _PROG_CACHE[_stage] = build_program(_stage)
    nc = _PROG_CACHE[_stage]
    res = run_bass_kernel_spmd(nc, in_maps, core_ids=list(range(NCORES)))
    r = res.results
    y_prompt = np.stack([r[c]["yp"] for c in range(NCORES)], 0)
    y_sample = np.concatenate([r[c]["ys"].reshape(16, 8, D) for c in range(NCORES)], 0)
    shp = np.stack([r[c]["shp"] for c in range(NCORES)], 0)[None]
    scp = np.stack([r[c]["scp"] for c in range(NCORES)], 0)[None]
    shs = np.concatenate([r[c]["shs"] for c in range(NCORES)], 0)[None]
    scs = np.concatenate([r[c]["scs"].reshape(16, 2, 512) for c in range(NCORES)], 0)[None]
    return (y_prompt.astype(f32), y_sample.astype(f32), shp.astype(f32), scp.astype(f32), shs.astype(f32), scs.astype(f32))
```

```python
import os
import numpy as np
from contextlib import ExitStack
import concourse.bass as bass
import concourse.mybir as mybir
from concourse.bass_utils import run_bass_kernel_spmd

F32 = mybir.dt.float32
F32R = mybir.dt.float32r
BF16 = mybir.dt.bfloat16
AF = mybir.ActivationFunctionType
ALU = mybir.AluOpType

NCORES = 8
P = 128
D = 1024
KD = 8
DFF = 2816
NJ = 22
NIN = 5632
TM = 1152
EPS = 1e-6
NSLOT = 6
FFN_GROUPS = [(0, 8), (8, 8), (16, 6)]
CQ, CF, CI, COG, CBG, CCG, CVV, CGA, CGB = 0, 512, 1024, 1536, 2048, 2560, 3072, 3584, 4608
PG1, PGM, PG2, PGF, PL0, PL1, PGH, PCW, NPAR = 0, 8, 16, 24, 32, 36, 40, 44, 56
C_ID, C_ONE, C_MP, C_MS, C_R64, C_RMIX, C_M2, C_CM2, NCST = 0, 128, 256, 384, 512, 1024, 1408, 1424, 1426

PASSES = [
    dict(T=1152, subs=[(0, 384), (384, 384), (768, 384)], npb=8, ptok0=0, sample=True),
    dict(T=1024, subs=[(0, 512), (512, 512)], npb=8, ptok0=1024, sample=False),
]


class Tok:
    __slots__ = ("sem", "val", "eng")

    def __init__(self, sem, val, eng):
        self.sem, self.val, self.eng = sem, val, eng


class Buf:
    __slots__ = ("name", "w", "r")

    def __init__(self, name=""):
        self.name, self.w, self.r = name, None, []


class Eng:
    def __init__(self, name, sem, is_pe=False):
        self.name, self.sem, self.is_pe = name, sem, is_pe
        self.count = 0
        self.ops = []
        self.seen = {}

    def wait(self, tok):
        if tok is None:
            return
        k = id(tok.sem)
        if self.seen.get(k, 0) >= tok.val:
            return
        self.seen[k] = tok.val
        sem, val = tok.sem, tok.val
        self.ops.append(lambda e: e.wait_ge(sem, val))


class DmaSem:
    def __init__(self, sem):
        self.sem, self.count = sem, 0


class Prog:
    def __init__(self, sems):
        self.pe = Eng("pe", sems["pe"], is_pe=True)
        self.act = Eng("act", sems["act"])
        self.dve = Eng("dve", sems["dve"])
        self.pool = Eng("pool", sems["pool"])
        self.sp = Eng("sp", sems["sp"])
        self.bufs = {}

    def B(self, *key):
        b = self.bufs.get(key)
        if b is None:
            b = self.bufs[key] = Buf(str(key))
        return b

    def _deps(self, eng, reads, writes):
        for b in reads:
            if b.w is not None and not (eng.is_pe and b.w.eng is eng):
                eng.wait(b.w)
        for b in writes:
            for t in b.r:
                if t.eng is not eng or not eng.is_pe:
                    eng.wait(t)
            if b.w is not None and (b.w.eng is not eng or not eng.is_pe):
                eng.wait(b.w)

    @staticmethod
    def _commit(tok, reads, writes):
        for b in reads:
            b.r.append(tok)
        for b in writes:
            b.w = tok
            b.r = []

    def op(self, eng, fn, reads=(), writes=()):
        self._deps(eng, reads, writes)
        eng.count += 1
        sem, tok = eng.sem, Tok(eng.sem, eng.count, eng)
        eng.ops.append(lambda e: fn(e).then_inc(sem, 1))
        self._commit(tok, reads, writes)
        return tok

    def group(self, eng, fns, reads=(), writes=()):
        self._deps(eng, reads, writes)
        eng.count += 1
        sem, tok = eng.sem, Tok(eng.sem, eng.count, eng)
        for fn in fns[:-1]:
            eng.ops.append(fn)
        last = fns[-1]
        eng.ops.append(lambda e: last(e).then_inc(sem, 1))
        self._commit(tok, reads, writes)
        return tok

    def dma(self, eng, dsem, fn, reads=(), writes=()):
        self._deps(eng, reads, writes)
        dsem.count += 16
        tok = Tok(dsem.sem, dsem.count, None)
        sem = dsem.sem
        eng.ops.append(lambda e: fn(e).then_inc(sem, 16))
        self._commit(tok, reads, writes)
        return tok

    def replay(self, block):
        def mk(eng):
            def body(e):
                for f in eng.ops:
                    f(e)
            return body
        block.tensor(mk(self.pe))
        block.scalar(mk(self.act))
        block.vector(mk(self.dve))
        block.gpsimd(mk(self.pool))
        block.sync(mk(self.sp))


class WRing:
    def __init__(self, pg, arena, dsems, plan):
        self.pg, self.arena, self.dsems, self.plan = pg, arena, dsems, plan
        self.bufs = [Buf("wslot%d" % i) for i in range(NSLOT)]
        self.next_load = 0
        self.next_get = 0
        self.released = 0
        self.pump()

    def _view(self, idx):
        _, _, shape = self.plan[idx]
        n = shape[1] * shape[2]
        return self.arena[:, idx % NSLOT, 0:n].rearrange("p (k n) -> p k n", k=shape[1])

    def pump(self):
        while self.next_load < len(self.plan) and self.next_load < self.released + NSLOT:
            idx = self.next_load
            view = self._view(idx)
            src = self.plan[idx][1]
            self.pg.dma(self.pg.sp, self.dsems[idx % NSLOT],
                        lambda e, view=view, src=src: e.dma_start(out=view, in_=src),
                        writes=[self.bufs[idx % NSLOT]])
            self.next_load += 1

    def get(self, key):
        idx = self.next_get
        assert self.plan[idx][0] == key, (self.plan[idx][0], key)
        assert idx < self.next_load, "weight tile not issued (ring too small)"
        self.next_get += 1
        return self._view(idx), self.bufs[idx % NSLOT]

    def release(self, n=1):
        self.released += n
        self.pump()


def wtile(w2d, row0, nk, col0, ncol=128):
    return w2d[row0:row0 + nk * 128, col0:col0 + ncol].rearrange("(k p) n -> p k n", p=128)


def build_program(stage=99):
    nc = bass.Bass("TRN2", target_bir_lowering=False)
    nc.dge_precook = False

    def din(name, shape, dt=F32):
        return nc.dram_tensor(name, shape, dt, kind="ExternalInput").ap()

    def dout(name, shape):
        return nc.dram_tensor(name, shape, F32, kind="ExternalOutput").ap()

    xp_d = din("xp", [2048, D])
    xs_d = din("xs", [128, D])
    sh_d = din("sh", [16, 4, 128, 128], F32R)
    sc_d = din("sc", [32, 512])
    par_d = din("par", [128, NPAR])
    cst_d = din("cst", [128, NCST])
    w1a_d = din("w1a", [D, DFF], F32R)
    w3a_d = din("w3a", [D, DFF], F32R)
    w2a_d = din("w2a", [DFF, D], F32R)
    win_d = din("win", [D, NIN], F32R)
    wab_d = din("wab", [1024, D], F32R)
    wo_d = din("wo", [D, D], F32R)
    w1b_d = din("w1b", [D, DFF], F32R)
    w3b_d = din("w3b", [D, DFF], F32R)
    w2b_d = din("w2b", [DFF, D], F32R)
    yp_d = dout("yp", [2048, D])
    ys_d = dout("ys", [128, D])
    shp_d = dout("shp", [4, 128, 128])
    scp_d = dout("scp", [2, 512])
    shs_d = dout("shs", [16, 4, 128, 128])
    scs_d = dout("scs", [32, 512])

    def ffn_plan(tag, w1, w3, w2):
        pl = []
        for g, (j0, nj) in enumerate(FFN_GROUPS):
            for j in range(j0, j0 + nj):
                pl.append(((tag, "w1", j), wtile(w1, 0, 8, j * 128), (128, 8, 128)))
                pl.append(((tag, "w3", j), wtile(w3, 0, 8, j * 128), (128, 8, 128)))
            for m in range(8):
                pl.append(((tag, "w2", g, m), wtile(w2, j0 * 128, nj, m * 128), (128, nj, 128)))
        return pl

    def mixer_plan():
        pl = []
        for kp in range(4):
            pl.append((("wv", kp), wtile(win_d, kp * 256, 2, CI, 512), (128, 2, 512)))
        if stage < 2.2:
            return pl
        for h in range(4):
            pl.append((("wq", h), wtile(win_d, 0, 8, CQ + h * 128), (128, 8, 128)))
            pl.append((("wf", h), wtile(win_d, 0, 8, CF + h * 128), (128, 8, 128)))
            pl.append((("wog", h), wtile(win_d, 0, 8, COG + h * 128), (128, 8, 128)))
        if stage < 2.3:
            return pl
        for c in range(4):
            pl.append((("wbg", c), wtile(win_d, 0, 8, CBG + c * 128), (128, 8, 128)))
            pl.append((("wcg", c), wtile(win_d, 0, 8, CCG + c * 128), (128, 8, 128)))
            pl.append((("wvv", c), wtile(win_d, 0, 8, CVV + c * 128), (128, 8, 128)))
        if stage < 2.4:
            return pl
        for qt in range(4):
            for mm in range(2):
                m = qt * 2 + mm
                pl.append((("wga", m), wtile(win_d, 0, 8, CGA + m * 128), (128, 8, 128)))
                pl.append((("wgb", m), wtile(win_d, 0, 8, CGB + m * 128), (128, 8, 128)))
                pl.append((("wab", m), wtile(wab_d, 0, 8, m * 128), (128, 8, 128)))
            for mo in range(8):
                pl.append((("wo", qt, mo), wtile(wo_d, qt * 256, 2, mo * 128), (128, 2, 128)))
        return pl

    plan = []
    for ip in range(len(PASSES)):
        if stage >= 1:
            plan += ffn_plan("a", w1a_d, w3a_d, w2a_d)
        if stage >= 2:
            plan += mixer_plan()
        if stage >= 3:
            plan += ffn_plan("b", w1b_d, w3b_d, w2b_d)

    es = ExitStack()
    with es:
        def sb(name, shape, dt):
            return es.enter_context(nc.sbuf_tensor("sb_" + name, shape, dt))

        def newsem(name):
            return es.enter_context(nc.semaphore(name))

        xT = sb("xT", [128, KD, TM], F32)
        hT = sb("hT", [128, KD, TM], F32R)
        WB = sb("WB", [128, 10, TM], F32R)
        arena = sb("arena", [128, NSLOT, 1024], F32R)
        vtok = sb("vtok", [128, 9, 512], BF16)
        NTMP = 8
        tmp = [sb("tmp%d" % i, [128, 520], F32) for i in range(NTMP)]
        xst = [sb("xst%d" % i, [128, D], F32) for i in range(2)]
        s0t = sb("s0t", [128, 8, 128], F32R)
        snw = sb("snw", [128, 8, 128], F32)
        vblk = sb("vblk", [128, 8, 128], BF16)
        S = sb("S", [128, 4, 128], F32)
        kdb = [sb("kdb%d" % i, [128, 512], BF16) for i in range(2)]
        qdb = [sb("qdb%d" % i, [128, 512], BF16) for i in range(2)]
        kkb = [sb("kkb%d" % i, [128, 512], BF16) for i in range(2)]
        cm2b = sb("cm2b", [128, 2], BF16)
        osq = sb("osq", [128, 512], F32R)
        rrow = sb("rrow", [128, 4], F32R)
        ms4 = sb("ms4", [128, 8], F32)
        identr = sb("identr", [128, 128], F32R)
        par = sb("par", [128, NPAR], F32)
        cst = sb("cst", [128, NCST], F32)
        drv = sb("drv", [128, 16], F32)
        identb = sb("identb", [128, 128], BF16)
        onesr = sb("onesr", [128, 128], F32R)
        maskP4 = sb("maskP4", [128, 512], BF16)
        maskMix = sb("maskMix", [128, 384], BF16)
        m2b = sb("m2b", [128, 16], BF16)
        neghalf = sb("neghalf", [128, 8], F32)
        ucar = sb("ucar", [128, 4, 2], F32)
        usmp = sb("usmp", [128, 4, 16, 10], F32)
        ps = es.enter_context(nc.psum_tensor("ps", [128, 8, 512], F32))
        scst = tmp[6][0:32, 0:512]
        kkT = [xst[0][:, j_ * 256:(j_ + 1) * 256].bitcast(BF16) for j_ in range(2)]
        scmb = [xst[0][:, 512 + j_ * 256:512 + (j_ + 1) * 256].bitcast(BF16) for j_ in range(2)]
        sco = tmp[7][0:32, 0:512]

        sems = {n: newsem("s_" + n) for n in ["pe", "act", "dve", "pool", "sp"]}
        pg = Prog(sems)
        B = pg.B
        wsems = [DmaSem(newsem("w%d" % i)) for i in range(NSLOT)]
        NXS = 5
        xsem = [[DmaSem(newsem("xl%d_%d" % (i, hf_))) for hf_ in range(2)] for i in range(NXS)]
        ysem = [DmaSem(newsem("ys%d" % i)) for i in range(2)]
        msem = DmaSem(newsem("misc"))
        s0sem = DmaSem(newsem("s0l"))
        snsem = DmaSem(newsem("sns"))
        s0sem2 = [DmaSem(newsem("s0l%d" % i)) for i in range(2)]
        snsem2 = [DmaSem(newsem("sns%d" % i)) for i in range(2)]
        osem = DmaSem(newsem("outs"))
        block = es.enter_context(nc.Block())

        pe, act, dve, pool, sp = pg.pe, pg.act, pg.dve, pg.pool, pg.sp
        PSB = [B("ps", i) for i in range(8)]
        TB = [B("tmp", i) for i in range(NTMP)]
        ident = cst[:, C_ID:C_ID + 128]
        CST = B("cst")

        def pcol(c):
            return par[:, c:c + 1]

        pg.dma(sp, msem, lambda e: e.dma_start(out=par[:], in_=par_d), writes=[B("par")])
        pg.dma(sp, msem, lambda e: e.dma_start(out=cst[:], in_=cst_d), writes=[CST])
        t_last = pg.dma(sp, msem, lambda e: e.dma_start(out=scst, in_=sc_d), writes=[TB[6]])
        for b_ in [B("par"), CST, TB[6]]:
            b_.w = t_last
        W = None

        pg.op(pool, lambda e: e.memset(neghalf[:], -0.5), writes=[B("neghalf")])
        pg.op(pool, lambda e: e.memset(S[:], 0.0), writes=[B("S")])
        pg.op(pool, lambda e: e.memset(ucar[:], 0.0), writes=[B("ucar")])
        pg.op(dve, lambda e: e.tensor_copy(identb[:], ident), reads=[CST], writes=[B("identb")])
        pg.op(dve, lambda e: e.tensor_copy(onesr[:], cst[:, C_ONE:C_ONE + 128]), reads=[CST], writes=[B("onesr")])
        pg.op(dve, lambda e: e.tensor_copy(identr[:], ident), reads=[CST], writes=[B("identr")])
        for i_ in range(4):
            pg.op(dve, lambda e, i_=i_: e.tensor_copy(maskP4[:, i_ * 128:(i_ + 1) * 128], cst[:, C_MP:C_MP + 128]), reads=[CST], writes=[B("masks")])
        for i_ in range(3):
            c_ = C_MP if i_ < 2 else C_MS
            pg.op(dve, lambda e, i_=i_, c_=c_: e.tensor_copy(maskMix[:, i_ * 128:(i_ + 1) * 128], cst[:, c_:c_ + 128]), reads=[CST], writes=[B("masks")])
        pg.op(dve, lambda e: e.tensor_copy(m2b[:], cst[:, C_M2:C_M2 + 16]), reads=[CST], writes=[B("m2b")])
        pg.op(dve, lambda e: e.tensor_copy(cm2b[:], cst[:, C_CM2:C_CM2 + 2]), reads=[CST], writes=[B("cm2b")])
        pg.op(dve, lambda e: e.tensor_tensor(out=drv[:, 8:12], in0=par[:, PL0:PL0 + 4], in1=par[:, PL1:PL1 + 4], op=ALU.subtract),
              reads=[B("par")], writes=[B("drv")])
        pg.op(act, lambda e: e.activation(out=drv[:, 12:16], in_=drv[:, 8:12], func=AF.Tanh, scale=0.5),
              reads=[B("drv")], writes=[B("drv")])
        pg.op(dve, lambda e: e.tensor_scalar(out=drv[:, 0:4], in0=drv[:, 12:16], scalar1=-0.25, scalar2=0.25, op0=ALU.mult, op1=ALU.add),
              reads=[B("drv")], writes=[B("drv")])
        pg.op(dve, lambda e: e.tensor_scalar(out=drv[:, 4:8], in0=drv[:, 0:4], scalar1=-1.0, scalar2=None, op0=ALU.mult),
              reads=[B("drv")], writes=[B("drv")])
        pg.op(dve, lambda e: e.tensor_scalar(out=drv[:, 8:12], in0=drv[:, 0:4], scalar1=-1.0, scalar2=1.0, op0=ALU.mult, op1=ALU.add),
              reads=[B("drv")], writes=[B("drv")])
        DRV = B("drv")

        def load_sample_conv_state():
            fns = []
            for c in range(4):
                fns.append(lambda e, c=c: e.transpose(ps[:, 7, c * 32:(c + 1) * 32], scst[:, c * 128:(c + 1) * 128], cst[0:32, C_ID:C_ID + 32]))
            pg.group(pe, fns, reads=[TB[6], CST], writes=[PSB[7]])
            for c in range(4):
                pg.op(dve, lambda e, c=c: e.tensor_copy(usmp[:, c, :, 0:2], ps[:, 7, c * 32:(c + 1) * 32].rearrange("p (i t) -> p i t", t=2)),
                      reads=[PSB[7]], writes=[B("usmp", c)])

        def xslot(sl, hf):
            if sl < 2:
                return xst[sl][:, hf * 512:(hf + 1) * 512], B("xst", sl, hf)
            a_ = 2 * (sl - 2) + hf
            return tmp[a_][:, 0:512], TB[a_]

        def load_x(ip):
            pa = PASSES[ip]
            nblk = pa["T"] // 128
            for b in range(nblk):
                sl = b % NXS
                for hf in range(2):
                    ap_, buf_ = xslot(sl, hf)
                    if b < pa["npb"]:
                        r0 = pa["ptok0"] + b * 128
                        src = xp_d[r0:r0 + 128, hf * 512:(hf + 1) * 512]
                    else:
                        src = xs_d[:, hf * 512:(hf + 1) * 512]
                    pg.dma(sp, xsem[sl][hf], lambda e, ap_=ap_, src=src: e.dma_start(out=ap_, in_=src), writes=[buf_])
                for hf in range(2):
                    ap_, buf_ = xslot(sl, hf)
                    bank = 4 + 2 * (b % 2) + hf
                    fns = [lambda e, k=k, ap_=ap_, bank=bank: e.transpose(ps[:, bank, (k % 4) * 128:(k % 4 + 1) * 128],
                                                                          ap_[:, (k % 4) * 128:(k % 4 + 1) * 128], ident)
                           for k in range(hf * 4, hf * 4 + 4)]
                    pg.group(pe, fns, reads=[buf_, CST], writes=[PSB[bank]])
                    dst = xT[:, hf * 4:hf * 4 + 4, b * 128:(b + 1) * 128]
                    src_ps = ps[:, bank, :].rearrange("p (k n) -> p k n", k=4)
                    wr = [B("xT", k, b) for k in range(hf * 4, hf * 4 + 4)]
                    if hf == 0:
                        pg.op(act, lambda e, dst=dst, src_ps=src_ps: e.activation(out=dst, in_=src_ps, func=AF.Copy),
                              reads=[PSB[bank]], writes=wr)
                    else:
                        pg.op(dve, lambda e, dst=dst, src_ps=src_ps: e.tensor_copy(dst, src_ps),
                              reads=[PSB[bank]], writes=wr)

        def mm(out, lhsT, rhs, start, stop, skip=False):
            return lambda e: e.matmul(out, lhsT=lhsT, rhs=rhs, start=start, stop=stop, skip_group_check=skip)

        def proj(wv_, wb_, bank, si, c0, n):
            pg.group(pe, [mm(ps[:, bank, 0:n], wv_[:, k, :], hT[:, k, c0:c0 + n], k == 0, k == KD - 1) for k in range(KD)],
                     reads=[wb_, B("hT", si)], writes=[PSB[bank]])

        def rstd_bcast(sqs, sqbufs, n, inv_dim, ss_bank, out_bank, split=False):
            nb = n // 128
            nk = len(sqs)
            if nk > 1:
                pg.group(pe, [mm(ps[:, ss_bank, 0:n], onesr[:], sqs[k], k == 0, k == nk - 1) for k in range(nk)],
                         reads=list(sqbufs) + [B("onesr")], writes=[PSB[ss_bank]])
                pg.op(act, lambda e: e.activation(out=osq[0:1, 0:n], in_=ps[0:1, ss_bank, 0:n], func=AF.Copy),
                      reads=[PSB[ss_bank]], writes=[B("osq")])
                pg.group(pe, [mm(ps[:, ss_bank, 2 * blk:2 * blk + 2], osq[0:1, blk * 128:(blk + 1) * 128], onesr[0:1, 0:2], True, True, True)
                              for blk in range(nb)], reads=[B("osq"), B("onesr")], writes=[PSB[ss_bank]])
            else:
                fns = []
                for blk in range(nb):
                    for k in range(nk):
                        fns.append(mm(ps[:, ss_bank, 2 * blk:2 * blk + 2], sqs[k][:, blk * 128:(blk + 1) * 128], onesr[:, 0:2], k == 0, k == nk - 1, True))
                pg.group(pe, fns, reads=list(sqbufs) + [B("onesr")], writes=[PSB[ss_bank]])
            pg.op(dve, lambda e: e.tensor_scalar(out=ms4[:, 0:nb], in0=ps[:, ss_bank, 0:2 * nb].rearrange("p (b t) -> p b t", t=2)[:, :, 0],
                                                  scalar1=inv_dim, scalar2=EPS, op0=ALU.mult, op1=ALU.add), reads=[PSB[ss_bank]], writes=[B("ms4")])
            pg.op(pool, lambda e: e.tensor_tensor(out=rrow[:, 0:nb], in0=ms4[:, 0:nb], in1=neghalf[:, 0:nb], op=ALU.pow),
                  reads=[B("ms4"), B("neghalf")], writes=[B("rrow")])
            if not split:
                rstd_b(n, out_bank)

        def rstd_b(n, out_bank):
            nb = n // 128
            pg.group(pe, [mm(ps[:, out_bank, blk * 128:(blk + 1) * 128], rrow[:, blk:blk + 1].broadcast_to([128, 128]), identr[:], True, True, True)
                          for blk in range(nb)],
                     reads=[B("rrow"), B("identr")], writes=[PSB[out_bank]])

        def norm(ip, gcol, dst_is_h=True):
            pa = PASSES[ip]
            for si, (c0, n) in enumerate(pa["subs"]):
                blks = range(c0 // 128, (c0 + n) // 128)
                for k in range(KD):
                    pg.op(act, lambda e, k=k, c0=c0, n=n: e.activation(out=WB[:, k, c0:c0 + n], in_=xT[:, k, c0:c0 + n], func=AF.Square),
                          reads=[B("xT", k, b) for b in blks], writes=[B("WB", k, si)])
                rstd_bcast([WB[:, k, c0:c0 + n] for k in range(KD)], [B("WB", k, si) for k in range(KD)], n, 1.0 / D, 6, 7)
                for k in range(KD):
                    pg.op(dve, lambda e, k=k, c0=c0, n=n: e.scalar_tensor_tensor(out=hT[:, k, c0:c0 + n], in0=xT[:, k, c0:c0 + n],
                                                                                 scalar=pcol(gcol + k), in1=ps[:, 7, 0:n],
                                                                                 op0=ALU.mult, op1=ALU.mult),
                          reads=[B("xT", k, b) for b in blks] + [PSB[7], B("par")], writes=[B("hT", si)])

        def xupdate(ip, m, si, c0, n, bank, scale):
            blks = range(c0 // 128, (c0 + n) // 128)
            xb = [B("xT", m, b) for b in blks]
            pg.op(dve, lambda e: e.scalar_tensor_tensor(out=xT[:, m, c0:c0 + n], in0=ps[:, bank, 0:n], scalar=scale,
                                                        in1=xT[:, m, c0:c0 + n], op0=ALU.mult, op1=ALU.add),
                  reads=[PSB[bank]] + xb, writes=xb)

        def ffn(ip, tag):
            pa = PASSES[ip]
            it = 0
            ity = 0
            for g, (j0, nj) in enumerate(FFN_GROUPS):
                for jj in range(nj):
                    j = j0 + jj
                    w1v, w1b = W.get((tag, "w1", j))
                    w3v, w3b = W.get((tag, "w3", j))
                    for si, (c0, n) in enumerate(pa["subs"]):
                        b1, b3 = it % 2, 2 + it % 2
                        st = tmp[it % 2]
                        it += 1
                        pg.group(pe, [lambda e, k=k, w1v=w1v, b1=b1, c0=c0, n=n: e.matmul(ps[:, b1, 0:n], lhsT=w1v[:, k, :], rhs=hT[:, k, c0:c0 + n],
                                                                                     start=(k == 0), stop=(k == KD - 1)) for k in range(KD)],
                                 reads=[w1b, B("hT", si)], writes=[PSB[b1]])
                        pg.group(pe, [lambda e, k=k, w3v=w3v, b3=b3, c0=c0, n=n: e.matmul(ps[:, b3, 0:n], lhsT=w3v[:, k, :], rhs=hT[:, k, c0:c0 + n],
                                                                                     start=(k == 0), stop=(k == KD - 1)) for k in range(KD)],
                                 reads=[w3b, B("hT", si)], writes=[PSB[b3]])
                        pg.op(act, lambda e, st=st, b1=b1, n=n: e.activation(out=st[:, 0:n], in_=ps[:, b1, 0:n], func=AF.Silu),
                              reads=[PSB[b1]], writes=[TB[(it - 1) % 2]])
                        pg.op(dve, lambda e, st=st, b3=b3, jj=jj, c0=c0, n=n: e.tensor_tensor(out=WB[:, jj, c0:c0 + n], in0=st[:, 0:n],
                                                                                             in1=ps[:, b3, 0:n], op=ALU.mult),
                              reads=[TB[(it - 1) % 2], PSB[b3]], writes=[B("WB", jj, si)])
                    W.release(2)
                for m in range(8):
                    w2v, w2b = W.get((tag, "w2", g, m))
                    for si, (c0, n) in enumerate(pa["subs"]):
                        by = 4 + ity % 4
                        ity += 1
                        pg.group(pe, [lambda e, jj=jj, w2v=w2v, by=by, c0=c0, n=n, nj=nj: e.matmul(ps[:, by, 0:n], lhsT=w2v[:, jj, :], rhs=WB[:, jj, c0:c0 + n],
                                                                                            start=(jj == 0), stop=(jj == nj - 1)) for jj in range(nj)],
                                 reads=[w2b] + [B("WB", jj, si) for jj in range(nj)], writes=[PSB[by]])
                        xupdate(ip, m, si, c0, n, by, 0.5)
                    W.release(1)

        def final(ip):
            pa = PASSES[ip]
            for si, (c0, n) in enumerate(pa["subs"]):
                blks = range(c0 // 128, (c0 + n) // 128)
                for k in range(KD):
                    pg.op(act, lambda e, k=k, c0=c0, n=n: e.activation(out=WB[:, k, c0:c0 + n], in_=xT[:, k, c0:c0 + n], func=AF.Square),
                          reads=[B("xT", k, b) for b in blks], writes=[B("WB", k, si)])
                rstd_bcast([WB[:, k, c0:c0 + n] for k in range(KD)], [B("WB", k, si) for k in range(KD)], n, 1.0 / D, 5, 4)
                for k in range(KD):
                    pg.op(dve, lambda e, k=k, c0=c0, n=n: e.scalar_tensor_tensor(out=tmp[k][:, 0:n], in0=xT[:, k, c0:c0 + n],
                                                                                 scalar=pcol(PGF + k), in1=ps[:, 4, 0:n],
                                                                                 op0=ALU.mult, op1=ALU.mult),
                          reads=[B("xT", k, b) for b in blks] + [PSB[4], B("par")], writes=[TB[k]])
                for b in blks:
                    bc = b * 128
                    st = xst[b % 2]
                    STB = [B("xst", b % 2, 0), B("xst", b % 2, 1)]
                    for hf in range(2):
                        bank = [6, 7, 2, 3][(b % 2) * 2 + hf]
                        pg.group(pe, [lambda e, k=k, bank=bank, lc=bc - c0: e.transpose(ps[:, bank, (k % 4) * 128:(k % 4 + 1) * 128],
                                                                                         tmp[k][:, lc:lc + 128], ident)
                                      for k in range(hf * 4, hf * 4 + 4)], reads=[TB[k] for k in range(hf * 4, hf * 4 + 4)] + [CST], writes=[PSB[bank]])
                        if hf == 0:
                            pg.op(act, lambda e, st=st, bank=bank: e.activation(out=st[:, 0:512], in_=ps[:, bank, :], func=AF.Copy),
                                  reads=[PSB[bank]], writes=[STB[0]])
                        else:
                            pg.op(dve, lambda e, st=st, bank=bank: e.tensor_copy(st[:, 512:1024], ps[:, bank, :]),
                                  reads=[PSB[bank]], writes=[STB[1]])
                    if b < pa["npb"]:
                        r0 = pa["ptok0"] + b * 128
                        dst = yp_d[r0:r0 + 128, :]
                    else:
                        dst = ys_d[:, :]
                    pg.dma(sp, ysem[b % 2], lambda e, st=st, dst=dst: e.dma_start(out=dst, in_=st[:]), reads=STB)

        def mixer(ip):
            pa = PASSES[ip]
            T, subs, npb, sample = pa["T"], pa["subs"], pa["npb"], pa["sample"]
            nblk = T // 128
            last_pass = (ip == len(PASSES) - 1)
            norm(ip, PGM)

            def sub_of(b):
                for si_, (c0_, n_) in enumerate(subs):
                    if c0_ <= b * 128 < c0_ + n_:
                        return si_

            wv = [W.get(("wv", kp)) for kp in range(4)]
            for b in range(nblk):
                bank = b % 2
                pg.group(pe, [mm(ps[:, bank, :], hT[:, k, b * 128:(b + 1) * 128], wv[k // 2][0][:, k % 2, :], k == 0, k == KD - 1)
                              for k in range(KD)],
                         reads=[B("hT", sub_of(b))] + [x[1] for x in wv], writes=[PSB[bank]])
                if b % 2 == 0:
                    pg.op(act, lambda e, b=b, bank=bank: e.activation(out=vtok[:, b, :], in_=ps[:, bank, :], func=AF.Copy),
                          reads=[PSB[bank]], writes=[B("vtok", b)])
                else:
                    pg.op(dve, lambda e, b=b, bank=bank: e.tensor_copy(vtok[:, b, :], ps[:, bank, :]),
                          reads=[PSB[bank]], writes=[B("vtok", b)])
            W.release(4)

            if stage < 2.2:
                return
            psb1 = ps[:, 1, :].bitcast(BF16)
            NRING = 24

            def SH(e_):
                e_ = e_ % NRING
                return WB[:, 5 + e_ // 8, (e_ % 8) * 128:(e_ % 8 + 1) * 128]

            st = dict(bi=0, ring=0, ap=0, a3=0)
            items = [(h, si) for h in range(4) for si in range(len(subs))]
            wts = {}
            info = {}

            def stage_A(h, si):
                c0, n = subs[si]
                if si == 0:
                    wts[h] = (W.get(("wq", h)), W.get(("wf", h)), W.get(("wog", h)))
                    e0 = st["ring"]
                    st["ring"] += 1
                    pg.op(dve, lambda e, h=h, e0=e0: e.tensor_copy(SH(e0), S[:, h, :]), reads=[B("S", h), B("S")], writes=[B("SH", e0 % NRING)])
                    info[("cur", h)] = e0
                (wq, wqb), (wf, wfb), (wg, wgb) = wts[h]
                ap = st["ap"] % 2
                st["ap"] += 1
                has_s = sample and (c0 + n > npb * 128)
                npc = (npb * 128 - c0) if has_s else n
                Pt, PB_ = (tmp[6], TB[6]) if ap == 0 else (tmp[7], TB[7])
                a3 = st["a3"] % 3
                st["a3"] += 1
                sg, SGB = [(tmp[1], TB[1]), (tmp[2], TB[2]), (tmp[3], TB[3])][a3]
                kd_, KDB = kdb[ap], B("kdb", ap)
                qd_, QDB = qdb[ap], B("qdb", ap)
                kk_, KKB = kkb[ap], B("kkb", ap)
                qr_, QRB = WB[:, [8, 9, 4][a3], 0:512], B("qdR", a3)
                info[(h, si)] = dict(ap=ap, Pt=Pt, PB_=PB_, sg=sg, SGB=SGB, kd_=kd_, KDB=KDB, qd_=qd_, QDB=QDB, kk_=kk_, KKB=KKB,
                                     qr_=qr_, QRB=QRB, has_s=has_s, npc=npc, obank=3 + ap)
                proj(wf, wfb, 1, si, c0, n)
                pg.op(act, lambda e: e.activation(out=tmp[0][:, 0:n], in_=ps[:, 1, 0:n], func=AF.Tanh, scale=0.5),
                      reads=[PSB[1]], writes=[TB[0]])
                pg.op(act, lambda e: e.activation(out=tmp[4][:, 0:n], in_=tmp[0][:, 0:n], func=AF.Identity, scale=drv[:, 4 + h:5 + h],
                                                  bias=drv[:, h:h + 1]), reads=[TB[0], DRV], writes=[TB[4]])
                pg.op(act, lambda e: e.activation(out=tmp[5][:, 0:n], in_=tmp[0][:, 0:n], func=AF.Identity, scale=drv[:, h:h + 1],
                                                  bias=drv[:, 8 + h:9 + h]), reads=[TB[0], DRV], writes=[TB[5]])
                yield
                proj(wq, wqb, 0, si, c0, n)
                yield
                proj(wg, wgb, 2, si, c0, n)
                if si == len(subs) - 1:
                    W.release(3)
                pg.op(act, lambda e: e.activation(out=sg[:, 0:n], in_=ps[:, 2, 0:n], func=AF.Silu),
                      reads=[PSB[2]], writes=[SGB])
                yield
                rm0 = C_RMIX if has_s else C_R64
                pg.op(dve, lambda e: e.tensor_tensor_scan(out=Pt[:, 0:n], data0=cst[:, rm0:rm0 + n], data1=tmp[5][:, 0:n],
                                                          initial=1.0, op0=ALU.max, op1=ALU.mult),
                      reads=[TB[5], CST], writes=[PB_])
                yield
                if False:
                    pg.op(pool, lambda e: e.tensor_tensor(out=kd_[:, 0:n], in0=tmp[4][:, 0:n], in1=Pt[:, 0:n], op=ALU.divide),
                          reads=[TB[4], PB_], writes=[KDB])
                else:
                    pg.op(dve, lambda e: e.reciprocal(tmp[5][:, 0:n], Pt[:, 0:n]), reads=[PB_], writes=[TB[5]])
                    pg.op(dve, lambda e: e.tensor_tensor(out=kd_[:, 0:n], in0=tmp[4][:, 0:n], in1=tmp[5][:, 0:n], op=ALU.mult),
                          reads=[TB[4], TB[5]], writes=[KDB])
                yield
                pg.op(dve, lambda e: e.tensor_tensor(out=qr_[:, 0:n], in0=ps[:, 0, 0:n], in1=Pt[:, 0:n], op=ALU.mult),
                      reads=[PSB[0], PB_], writes=[QRB])
                pg.op(dve, lambda e: e.tensor_copy(qd_[:, 0:n], qr_[:, 0:n].bitcast(F32)), reads=[QRB], writes=[QDB])
                yield
                nch = npc // 64
                pg.op(dve, lambda e: e.tensor_tensor(out=kk_[:, 0:npc].rearrange("p (c t) -> p c t", t=64),
                                                      in0=kd_[:, 0:npc].rearrange("p (c t) -> p c t", t=64),
                                                      in1=Pt[:, 0:npc].rearrange("p (c t) -> p c t", t=64)[:, :, 63:64].broadcast_to([128, nch, 64]),
                                                      op=ALU.mult), reads=[KDB, PB_], writes=[KKB])
                if has_s:
                    pg.op(dve, lambda e: e.tensor_tensor(out=kk_[:, npc:n].rearrange("p (c t) -> p c t", t=8),
                                                          in0=kd_[:, npc:n].rearrange("p (c t) -> p c t", t=8),
                                                          in1=Pt[:, npc:n].rearrange("p (c t) -> p c t", t=8)[:, :, 7:8].broadcast_to([128, 16, 8]),
                                                          op=ALU.mult), reads=[KDB, PB_], writes=[KKB])

            def stage_B1(h, si):
                c0, n = subs[si]
                I = info[(h, si)]
                Pt, PB_, kd_, KDB, qd_, QDB, kk_, KKB, qr_, QRB, obank = (I["Pt"], I["PB_"], I["kd_"], I["KDB"], I["qd_"], I["QDB"],
                                                                           I["kk_"], I["KKB"], I["qr_"], I["QRB"], I["obank"])
                I["chunks"] = []
                blks = list(range(c0 // 128, (c0 + n) // 128))
                nb = len(blks)
                has_s = I["has_s"]
                npb_ = nb - 1 if has_s else nb
                st["bi"] += 1
                j = st["bi"] % 2
                kt, KTB = kkT[j], B("kkT", j)
                sm, SMB = scmb[j], B("scmb", j)
                b0 = blks[0]
                hc = slice(h * 128, (h + 1) * 128)
                if has_s:
                    src0 = sh_d[0:4, h].rearrange("s k v -> k s v")
                    pg.dma(sp, s0sem2[0], lambda e, src0=src0: e.dma_start(out=s0t[:, 0:4, :], in_=src0), writes=[B("s0t", 0)])
                pg.op(pool, lambda e: e.tensor_tensor(out=vblk[:, 0:2 * npb_, :].rearrange("p (b c) v -> p b c v", c=2),
                                                      in0=vtok[:, b0:b0 + npb_, hc].unsqueeze(2).broadcast_to([128, npb_, 2, 128]),
                                                      in1=cm2b[:, 0:2].unsqueeze(1).unsqueeze(3).broadcast_to([128, npb_, 2, 128]), op=ALU.mult),
                      reads=[B("vtok", b) for b in blks[:npb_]] + [B("cm2b")], writes=[B("vblk", 0)] + ([B("vblk", 1)] if npb_ > 2 else []))
                yield
                pg.group(pe, [lambda e, ib=ib: e.transpose(psb1[:, ib * 128:(ib + 1) * 128], kk_[:, ib * 128:(ib + 1) * 128], identb[:]) for ib in range(nb)],
                         reads=[KKB, B("identb")], writes=[PSB[1]])
                pg.op(act, lambda e: e.activation(out=kt[:, 0:nb * 128], in_=psb1[:, 0:nb * 128], func=AF.Copy), reads=[PSB[1]], writes=[KTB])
                yield
                pg.group(pe, [mm(ps[:, 5, ib * 128:(ib + 1) * 128], kd_[:, ib * 128:(ib + 1) * 128], qd_[:, ib * 128:(ib + 1) * 128], True, True, True)
                              for ib in range(nb)], reads=[KDB, QDB], writes=[PSB[5]])
                mt = maskMix if has_s else maskP4
                pg.op(dve, lambda e: e.tensor_tensor(out=sm[:, 0:nb * 128], in0=ps[:, 5, 0:nb * 128], in1=mt[:, 0:nb * 128], op=ALU.mult),
                      reads=[PSB[5], B("masks")], writes=[SMB])
                yield
                dbufs = [PSB[6]] + ([PSB[7]] if npb_ > 2 else [])
                pg.group(pe, [mm(ps[:, 6 + ib // 2, (ib % 2) * 256:(ib % 2) * 256 + 256], kt[:, ib * 128:(ib + 1) * 128], vblk[:, 2 * ib:2 * ib + 2, :], True, True, True)
                              for ib in range(npb_)], reads=[KTB, B("vblk", 0)] + ([B("vblk", 1)] if npb_ > 2 else []), writes=dbufs)
                pg.group(pe, [mm(ps[:, obank, ib * 128:(ib + 1) * 128], vtok[:, blks[ib], hc], sm[:, ib * 128:(ib + 1) * 128], ib == 0, False, True)
                              for ib in range(nb)], reads=[B("vtok", b) for b in blks] + [SMB], writes=[PSB[obank]])
                yield
                for ib in range(npb_):
                    lc = ib * 128
                    dbank = 6 + ib // 2
                    dcol = (ib % 2) * 256
                    for c2 in range(2):
                        cc = lc + c2 * 64
                        eprev = info[("cur", h)]
                        enew = st["ring"]
                        st["ring"] += 1
                        pg.op(dve, lambda e, eprev=eprev, enew=enew, cc=cc, dbank=dbank, dcol=dcol, c2=c2: e.scalar_tensor_tensor(
                            out=SH(enew), in0=SH(eprev).bitcast(F32), scalar=Pt[:, cc + 63:cc + 64],
                            in1=ps[:, dbank, dcol + c2 * 128:dcol + (c2 + 1) * 128], op0=ALU.mult, op1=ALU.add),
                            reads=[B("SH", eprev % NRING), PB_, PSB[dbank]], writes=[B("SH", enew % NRING)])
                        I["chunks"].append((cc, eprev, c2 == 1))
                        info[("cur", h)] = enew
                    yield
                if has_s:
                    ib = nb - 1
                    lc = ib * 128
                    b = blks[ib]
                    vsl = vtok[:, b, hc]
                    ktS = kt[:, ib * 128:(ib + 1) * 128]
                    for r4 in range(4):
                        jb = r4 % 2
                        if r4 + 1 < 4:
                            srcn = sh_d[(r4 + 1) * 4:(r4 + 2) * 4, h].rearrange("s k v -> k s v")
                            jn = (r4 + 1) % 2
                            pg.dma(sp, s0sem2[jn], lambda e, srcn=srcn, jn=jn: e.dma_start(out=s0t[:, jn * 4:jn * 4 + 4, :], in_=srcn),
                                   writes=[B("s0t", jn)])
                        pg.group(pe, [mm(ps[:, obank, lc + (r4 * 4 + i) * 8:lc + (r4 * 4 + i) * 8 + 8], s0t[:, jb * 4 + i, :],
                                         qr_[:, lc + (r4 * 4 + i) * 8:lc + (r4 * 4 + i) * 8 + 8], False, (r4 == 3 and i == 3), True) for i in range(4)],
                                 reads=[B("s0t", jb), QRB], writes=[PSB[obank]])
                        def vmask(rr):
                            hv = 1 - rr % 2
                            pg.op(pool, lambda e, rr=rr, hv=hv: e.tensor_tensor(
                                out=vblk[:, hv * 4:hv * 4 + 4, :], in0=vsl.unsqueeze(1).broadcast_to([128, 4, 128]),
                                in1=m2b[:, rr * 4:rr * 4 + 4].unsqueeze(2).broadcast_to([128, 4, 128]), op=ALU.mult),
                                reads=[B("vtok", b), B("m2b")], writes=[B("vblk", hv)])
                        if r4 == 0:
                            vmask(0)
                        if r4 + 1 < 4:
                            vmask(r4 + 1)
                        hv_ = 1 - r4 % 2
                        pg.group(pe, [mm(ps[:, 7, :], ktS, vblk[:, hv_ * 4:hv_ * 4 + 4, :], True, True, True)], reads=[KTB, B("vblk", hv_)], writes=[PSB[7]])
                        for i4 in range(4):
                            pl = lc + (r4 * 4 + i4) * 8 + 7
                            pg.op(dve, lambda e, jb=jb, i4=i4, pl=pl: e.scalar_tensor_tensor(
                                out=snw[:, jb * 4 + i4, :], in0=s0t[:, jb * 4 + i4, :].bitcast(F32), scalar=Pt[:, pl:pl + 1],
                                in1=ps[:, 7, i4 * 128:(i4 + 1) * 128], op0=ALU.mult, op1=ALU.add),
                                reads=[B("s0t", jb), PB_, PSB[7]], writes=[B("snw", jb)])
                        dst = shs_d[r4 * 4:(r4 + 1) * 4, h].rearrange("s k v -> k s v")
                        pg.dma(sp, snsem2[jb], lambda e, dst=dst, jb=jb: e.dma_start(out=dst, in_=snw[:, jb * 4:jb * 4 + 4, :]), reads=[B("snw", jb)])
                        yield
                if si == len(subs) - 1:
                    ecur = info[("cur", h)]
                    pg.op(dve, lambda e, ecur=ecur: e.tensor_copy(S[:, h, :], SH(ecur).bitcast(F32)), reads=[B("SH", ecur % NRING)], writes=[B("S", h)])

            def stage_B2G(h, si):
                c0, n = subs[si]
                I = info[(h, si)]
                obank, qr_, QRB, sg, SGB = I["obank"], I["qr_"], I["QRB"], I["sg"], I["SGB"]
                for (cc, eprev, last) in I["chunks"]:
                    pg.group(pe, [mm(ps[:, obank, cc:cc + 64], SH(eprev), qr_[:, cc:cc + 64], False, last, True)],
                             reads=[B("SH", eprev % NRING), QRB], writes=[PSB[obank]])
                yield
                pg.op(act, lambda e: e.activation(out=osq[:, 0:n], in_=ps[:, obank, 0:n], func=AF.Square), reads=[PSB[obank]], writes=[B("osq")])
                yield
                rstd_bcast([osq[:, 0:n]], [B("osq")], n, 1.0 / 128, 2, 2, split=True)
                yield
                yield
                rstd_b(n, 2)
                pg.op(dve, lambda e: e.tensor_tensor(out=sg[:, 0:n], in0=sg[:, 0:n], in1=ps[:, 2, 0:n], op=ALU.mult),
                      reads=[SGB, PSB[2]], writes=[SGB])
                pg.op(dve, lambda e: e.scalar_tensor_tensor(out=WB[:, h, c0:c0 + n], in0=ps[:, obank, 0:n], scalar=pcol(PGH + h),
                                                            in1=sg[:, 0:n], op0=ALU.mult, op1=ALU.mult),
                      reads=[PSB[obank], SGB, B("par")], writes=[B("WB", h, si)])

            def drive(gens):
                gens = [g for g in gens if g is not None]
                while gens:
                    for g in list(gens):
                        try:
                            next(g)
                        except StopIteration:
                            gens.remove(g)

            drive([stage_A(*items[0])])
            for idx in range(len(items)):
                gB2G = stage_B2G(*items[idx - 1]) if idx >= 1 else iter(())
                gB1 = stage_B1(*items[idx])
                gA = stage_A(*items[idx + 1]) if idx + 1 < len(items) else iter(())

                def step(g_, k_=1):
                    for _ in range(k_):
                        try:
                            next(g_)
                        except StopIteration:
                            return
                for g_ in (gA, gB2G, gB1, gB1, gB2G, gA, gB1, gB2G, gA, gA, gA, gB1):
                    step(g_)
                drive([gB1])
                drive([gA])
                drive([gB2G])
            drive([stage_B2G(*items[-1])])
            if last_pass:
                pg.dma(sp, osem, lambda e: e.dma_start(out=shp_d.rearrange("h k v -> k h v"), in_=S[:]),
                       reads=[B("S", h) for h in range(4)])

            if stage < 2.3:
                return
            itc = 0
            for c in range(4):
                wbg, wbgb = W.get(("wbg", c))
                wcg, wcgb = W.get(("wcg", c))
                wvv, wvvb = W.get(("wvv", c))
                w0, w1, w2 = pcol(PCW + c * 3 + 0), pcol(PCW + c * 3 + 1), pcol(PCW + c * 3 + 2)
                for si, (c0, n) in enumerate(subs):
                    itc += 1
                    bA, bB, bC = itc % 2, 2 + itc % 2, 4 + itc % 2
                    proj(wbg, wbgb, bA, si, c0, n)
                    proj(wcg, wcgb, bB, si, c0, n)
                    proj(wvv, wvvb, bC, si, c0, n)
                    npc = (min(c0 + n, npb * 128) - c0) if sample else n
                    has_s = n > npc
                    vvs, VVB = tmp[itc % 2], TB[itc % 2]
                    ut, UTB = tmp[4 + itc % 2], TB[4 + itc % 2]
                    pg.op(act, lambda e, vvs=vvs, bC=bC, n=n: e.activation(out=vvs[:, 0:n], in_=ps[:, bC, 0:n], func=AF.Copy),
                          reads=[PSB[bC]], writes=[VVB])
                    pg.op(pool, lambda e, ut=ut, c=c: e.tensor_copy(ut[:, 0:2], ucar[:, c, :]), reads=[B("ucar", c), B("ucar")], writes=[UTB])
                    pg.op(dve, lambda e, ut=ut, vvs=vvs, bB=bB, npc=npc: e.tensor_tensor(out=ut[:, 2:2 + npc], in0=ps[:, bB, 0:npc], in1=vvs[:, 0:npc], op=ALU.mult),
                          reads=[PSB[bB], VVB], writes=[UTB])
                    pg.op(pool, lambda e, ut=ut, c=c, npc=npc: e.tensor_copy(ucar[:, c, :], ut[:, npc:npc + 2]), reads=[UTB], writes=[B("ucar", c)])
                    pg.op(dve, lambda e, ut=ut, npc=npc, w2=w2: e.tensor_scalar(out=tmp[6][:, 0:npc], in0=ut[:, 2:2 + npc], scalar1=w2, scalar2=None, op0=ALU.mult),
                          reads=[UTB, B("par")], writes=[TB[6]])
                    pg.op(dve, lambda e, ut=ut, npc=npc, w1=w1: e.scalar_tensor_tensor(out=tmp[6][:, 0:npc], in0=ut[:, 1:1 + npc], scalar=w1, in1=tmp[6][:, 0:npc],
                                                                                      op0=ALU.mult, op1=ALU.add), reads=[UTB, TB[6]], writes=[TB[6]])
                    pg.op(dve, lambda e, ut=ut, npc=npc, w0=w0: e.scalar_tensor_tensor(out=tmp[6][:, 0:npc], in0=ut[:, 0:npc], scalar=w0, in1=tmp[6][:, 0:npc],
                                                                                      op0=ALU.mult, op1=ALU.add), reads=[UTB, TB[6]], writes=[TB[6]])
                    pg.op(dve, lambda e, bA=bA, npc=npc, c=c, c0=c0: e.tensor_tensor(out=WB[:, 4 + c, c0:c0 + npc], in0=ps[:, bA, 0:npc], in1=tmp[6][:, 0:npc], op=ALU.mult),
                          reads=[PSB[bA], TB[6]], writes=[B("WB", 4 + c, si)])
                    if has_s:
                        def v3(ap):
                            return ap.rearrange("p (i t) -> p i t", t=8)
                        us = usmp[:, c]
                        USB = B("usmp", c)
                        cvs = v3(tmp[7][:, 0:128])
                        pg.op(dve, lambda e, us=us, vvs=vvs, bB=bB, npc=npc, n=n: e.tensor_tensor(out=us[:, :, 2:10], in0=v3(ps[:, bB, npc:n]), in1=v3(vvs[:, npc:n]), op=ALU.mult),
                              reads=[PSB[bB], VVB, USB], writes=[USB])
                        pg.op(dve, lambda e, us=us, cvs=cvs, w2=w2: e.tensor_scalar(out=cvs, in0=us[:, :, 2:10], scalar1=w2, scalar2=None, op0=ALU.mult),
                              reads=[USB, B("par")], writes=[TB[7]])
                        pg.op(dve, lambda e, us=us, cvs=cvs, w1=w1: e.scalar_tensor_tensor(out=cvs, in0=us[:, :, 1:9], scalar=w1, in1=cvs, op0=ALU.mult, op1=ALU.add),
                              reads=[USB, TB[7]], writes=[TB[7]])
                        pg.op(dve, lambda e, us=us, cvs=cvs, w0=w0: e.scalar_tensor_tensor(out=cvs, in0=us[:, :, 0:8], scalar=w0, in1=cvs, op0=ALU.mult, op1=ALU.add),
                              reads=[USB, TB[7]], writes=[TB[7]])
                        pg.op(dve, lambda e, bA=bA, npc=npc, n=n, c=c, c0=c0: e.tensor_tensor(out=WB[:, 4 + c, c0 + npc:c0 + n], in0=ps[:, bA, npc:n], in1=tmp[7][:, 0:128], op=ALU.mult),
                              reads=[PSB[bA], TB[7]], writes=[B("WB", 4 + c, si)])
                W.release(3)
            if sample:
                for c in range(4):
                    pg.op(dve, lambda e, c=c: e.tensor_copy(tmp[6][:, c * 32:(c + 1) * 32].rearrange("p (i t) -> p i t", t=2), usmp[:, c, :, 8:10]),
                          reads=[B("usmp", c)], writes=[TB[6]])
                pg.group(pe, [lambda e, c=c: e.transpose(ps[0:32, 7, c * 128:(c + 1) * 128], tmp[6][:, c * 32:(c + 1) * 32], ident) for c in range(4)],
                         reads=[TB[6], CST], writes=[PSB[7]])
                pg.op(dve, lambda e: e.tensor_copy(sco, ps[0:32, 7, :]), reads=[PSB[7]], writes=[TB[7]])
                pg.dma(sp, osem, lambda e: e.dma_start(out=scs_d, in_=sco), reads=[TB[7]])
            if last_pass:
                pg.group(pe, [lambda e, c=c: e.transpose(ps[0:2, 7, c * 128:(c + 1) * 128], ucar[:, c, :], ident) for c in range(4)],
                         reads=[B("ucar", c) for c in range(4)] + [CST], writes=[PSB[7]])
                pg.op(dve, lambda e: e.tensor_copy(tmp[7][0:2, 0:512], ps[0:2, 7, :]), reads=[PSB[7]], writes=[TB[7]])
                pg.dma(sp, osem, lambda e: e.dma_start(out=scp_d, in_=tmp[7][0:2, 0:512]), reads=[TB[7]])

            if stage < 2.4:
                return
            it4 = 0
            ity = 0
            for qt in range(4):
                for mm_ in range(2):
                    m = qt * 2 + mm_
                    wga, wgab = W.get(("wga", m))
                    wgb, wgbb = W.get(("wgb", m))
                    wab, wabb = W.get(("wab", m))
                    wao, waob = wab[:, 0:4, :], wabb
                    wbo, wbob = wab[:, 4:8, :], wabb
                    for si, (c0, n) in enumerate(subs):
                        it4 += 1
                        st_ = it4 % 2
                        bk = [0, 1, 2, 3] if st_ == 0 else [4, 5, 6, 7]
                        proj(wga, wgab, bk[0], si, c0, n)
                        proj(wgb, wgbb, bk[1], si, c0, n)
                        pg.group(pe, [mm(ps[:, bk[2], 0:n], wao[:, kk, :], WB[:, kk, c0:c0 + n], kk == 0, kk == 3) for kk in range(4)],
                                 reads=[waob] + [B("WB", kk, si) for kk in range(4)], writes=[PSB[bk[2]]])
                        pg.group(pe, [mm(ps[:, bk[3], 0:n], wbo[:, kk, :], WB[:, 4 + kk, c0:c0 + n], kk == 0, kk == 3) for kk in range(4)],
                                 reads=[wbob] + [B("WB", 4 + kk, si) for kk in range(4)], writes=[PSB[bk[3]]])
                        ta, TAB = tmp[st_], TB[st_]
                        tb, TBB = tmp[2 + st_], TB[2 + st_]
                        t1, T1B = tmp[4 + st_], TB[4 + st_]
                        t2, T2B = tmp[6 + st_], TB[6 + st_]
                        pg.op(act, lambda e, ta=ta, b0=bk[0], n=n: e.activation(out=ta[:, 0:n], in_=ps[:, b0, 0:n], func=AF.Tanh, scale=0.5),
                              reads=[PSB[bk[0]]], writes=[TAB])
                        pg.op(act, lambda e, tb=tb, b1=bk[1], n=n: e.activation(out=tb[:, 0:n], in_=ps[:, b1, 0:n], func=AF.Tanh, scale=0.5),
                              reads=[PSB[bk[1]]], writes=[TBB])
                        pg.op(dve, lambda e, ta=ta, t1=t1, b2=bk[2], n=n: e.scalar_tensor_tensor(out=t1[:, 0:n], in0=ta[:, 0:n], scalar=1.0, in1=ps[:, b2, 0:n],
                                                                                              op0=ALU.add, op1=ALU.mult),
                              reads=[TAB, PSB[bk[2]]], writes=[T1B])
                        pg.op(dve, lambda e, tb=tb, t2=t2, b3=bk[3], n=n: e.scalar_tensor_tensor(out=t2[:, 0:n], in0=tb[:, 0:n], scalar=1.0, in1=ps[:, b3, 0:n],
                                                                                              op0=ALU.add, op1=ALU.mult),
                              reads=[TBB, PSB[bk[3]]], writes=[T2B])
                        pg.op(pool, lambda e, t1=t1, t2=t2, mm_=mm_, c0=c0, n=n: e.tensor_tensor(out=WB[:, 8 + mm_, c0:c0 + n], in0=t1[:, 0:n], in1=t2[:, 0:n], op=ALU.add),
                              reads=[T1B, T2B], writes=[B("WB", 8 + mm_, si)])
                    W.release(3)
                for mo in range(8):
                    wo_, wob = W.get(("wo", qt, mo))
                    for si, (c0, n) in enumerate(subs):
                        ity += 1
                        by = [0, 4, 1, 5][ity % 4]
                        pg.group(pe, [mm(ps[:, by, 0:n], wo_[:, kk, :], WB[:, 8 + kk, c0:c0 + n], kk == 0, kk == 1) for kk in range(2)],
                                 reads=[wob] + [B("WB", 8 + kk, si) for kk in range(2)], writes=[PSB[by]])
                        xupdate(ip, mo, si, c0, n, by, 0.5)
                    W.release(1)

        if PASSES[0]["sample"]:
            load_sample_conv_state()
        for ip in range(len(PASSES)):
            load_x(ip)
            if ip == 0:
                W = WRing(pg, arena, wsems, plan)
            if stage >= 1:
                norm(ip, PG1)
                ffn(ip, "a")
            if stage >= 2:
                mixer(ip)
            if stage >= 3:
                norm(ip, PG2)
                ffn(ip, "b")
            final(ip)

        for ds_ in ysem + [osem, snsem] + snsem2:
            if ds_.count:
                sp.wait(Tok(ds_.sem, ds_.count, None))
        pg.replay(block)
    return nc


def _consts():
    c = np.zeros((128, NCST), np.float32)
    idx = np.arange(128)
    c[:, C_ID:C_ID + 128] = np.eye(128, dtype=np.float32)
    c[:, C_ONE:C_ONE + 128] = 1.0
    s, t = idx[:, None], idx[None, :]
    c[:, C_MP:C_MP + 128] = ((s // 64 == t // 64) & (s <= t)).astype(np.float32)
    c[:, C_MS:C_MS + 128] = ((s // 8 == t // 8) & (s <= t)).astype(np.float32)
    r64 = (np.arange(512) % 64 == 0).astype(np.float32)
    c[:, C_R64:C_R64 + 512] = r64[None, :]
    rmix = np.concatenate([(np.arange(256) % 64 == 0), (np.arange(128) % 8 == 0)]).astype(np.float32)
    c[:, C_RMIX:C_RMIX + 384] = rmix[None, :]
    c[:, C_M2:C_M2 + 16] = (idx[:, None] // 8 == np.arange(16)[None, :]).astype(np.float32)
    c[:, C_CM2:C_CM2 + 2] = (idx[:, None] // 64 == np.arange(2)[None, :]).astype(np.float32)
    return c


def _fm(v, nchunk):
    return np.ascontiguousarray(np.asarray(v, np.float32).reshape(nchunk, 128).T)


_PROG_CACHE = {}


def kernel(x_prompt, x_sample, state_hgrn, state_conv, lower_bound_logits, g_ffn1, w1_ffn1, w3_ffn1, w2_ffn1,
           g_mix, w_in, conv_w, g_hgrn_out, w_a_out, w_b_out, w_o, g_ffn2, w1_ffn2, w3_ffn2, w2_ffn2, g_final,
           _stage=99):
    f32 = np.float32
    par = np.zeros((128, NPAR), f32)
    par[:, PG1:PG1 + 8] = _fm(g_ffn1[0], 8)
    par[:, PGM:PGM + 8] = _fm(g_mix[0], 8)
    par[:, PG2:PG2 + 8] = _fm(g_ffn2[0], 8)
    par[:, PGF:PGF + 8] = _fm(g_final, 8)
    par[:, PL0:PL0 + 4] = _fm(lower_bound_logits[0], 4)
    par[:, PL1:PL1 + 4] = _fm(lower_bound_logits[1], 4)
    par[:, PGH:PGH + 4] = _fm(g_hgrn_out[0], 4)
    for c in range(4):
        for j in range(3):
            par[:, PCW + c * 3 + j] = np.asarray(conv_w[0, j, c * 128:(c + 1) * 128], f32)
    cst = _consts()
    shared = dict(par=par, cst=cst,
                  w1a=np.ascontiguousarray(w1_ffn1[0], f32), w3a=np.ascontiguousarray(w3_ffn1[0], f32),
                  w2a=np.ascontiguousarray(w2_ffn1[0], f32), win=np.ascontiguousarray(w_in[0], f32),
                  wab=np.ascontiguousarray(np.concatenate([w_a_out[0], w_b_out[0]], axis=0), f32),
                  wo=np.ascontiguousarray(w_o[0], f32),
                  w1b=np.ascontiguousarray(w1_ffn2[0], f32), w3b=np.ascontiguousarray(w3_ffn2[0], f32),
                  w2b=np.ascontiguousarray(w2_ffn2[0], f32))
    in_maps = []
    for c in range(NCORES):
        m = dict(shared)
        m["xp"] = np.ascontiguousarray(x_prompt[c], f32)
        m["xs"] = np.ascontiguousarray(x_sample[c * 16:(c + 1) * 16], f32).reshape(128, D)
        m["sh"] = np.ascontiguousarray(state_hgrn[0, c * 16:(c + 1) * 16], f32)
        m["sc"] = np.ascontiguousarray(state_conv[0, c * 16:(c + 1) * 16], f32).reshape(32, 512)
        in_maps.append(m)
    if _stage not in _PROG_CACHE:
        _PROG_CACHE[_stage] = build_program(_stage)
    nc = _PROG_CACHE[_stage]
    res = run_bass_kernel_spmd(nc, in_maps, core_ids=list(range(NCORES)))
    r = res.results
    y_prompt = np.stack([r[c]["yp"] for c in range(NCORES)], 0)
    y_sample = np.concatenate([r[c]["ys"].reshape(16, 8, D) for c in range(NCORES)], 0)
    shp = np.stack([r[c]["shp"] for c in range(NCORES)], 0)[None]
    scp = np.stack([r[c]["scp"] for c in range(NCORES)], 0)[None]
    shs = np.concatenate([r[c]["shs"] for c in range(NCORES)], 0)[None]
    scs = np.concatenate([r[c]["scs"].reshape(16, 2, 512) for c in range(NCORES)], 0)[None]
    return (y_prompt.astype(f32), y_sample.astype(f32), shp.astype(f32), scp.astype(f32), shs.astype(f32), scs.astype(f32))
```

```python
import os
import numpy as np
from contextlib import ExitStack
import concourse.bass as bass
import concourse.mybir as mybir
from concourse.bass_utils import run_bass_kernel_spmd

F32 = mybir.dt.float32
F32R = mybir.dt.float32r
BF16 = mybir.dt.bfloat16
AF = mybir.ActivationFunctionType
ALU = mybir.AluOpType

NCORES = 8
P = 128
D = 1024
KD = 8
DFF = 2816
NJ = 22
NIN = 5632
TM = 1152
EPS = 1e-6
NSLOT = 6
FFN_GROUPS = [(0, 8), (8, 8), (16, 6)]
CQ, CF, CI, COG, CBG, CCG, CVV, CGA, CGB = 0, 512, 1024, 1536, 2048, 2560, 3072, 3584, 4608
PG1, PGM, PG2, PGF, PL0, PL1, PGH, PCW, NPAR = 0, 8, 16, 24, 32, 36, 40, 44, 56
C_ID, C_ONE, C_MP, C_MS, C_R64, C_RMIX, C_M2, C_CM2, NCST = 0, 128, 256, 384, 512, 1024, 1408, 1424, 1426

PASSES = [
    dict(T=1152, subs=[(0, 384), (384, 384), (768, 384)], npb=8, ptok0=0, sample=True),
    dict(T=1024, subs=[(0, 512), (512, 512)], npb=8, ptok0=1024, sample=False),
]


class Tok:
    __slots__ = ("sem", "val", "eng")

    def __init__(self, sem, val, eng):
        self.sem, self.val, self.eng = sem, val, eng


class Buf:
    __slots__ = ("name", "w", "r")

    def __init__(self, name=""):
        self.name, self.w, self.r = name, None, []


class Eng:
    def __init__(self, name, sem, is_pe=False):
        self.name, self.sem, self.is_pe = name, sem, is_pe
        self.count = 0
        self.ops = []
        self.seen = {}

    def wait(self, tok):
        if tok is None:
            return
        k = id(tok.sem)
        if self.seen.get(k, 0) >= tok.val:
            return
        self.seen[k] = tok.val
        sem, val = tok.sem, tok.val
        self.ops.append(lambda e: e.wait_ge(sem, val))


class DmaSem:
    def __init__(self, sem):
        self.sem, self.count = sem, 0


class Prog:
    def __init__(self, sems):
        self.pe = Eng("pe", sems["pe"], is_pe=True)
        self.act = Eng("act", sems["act"])
        self.dve = Eng("dve", sems["dve"])
        self.pool = Eng("pool", sems["pool"])
        self.sp = Eng("sp", sems["sp"])
        self.bufs = {}

    def B(self, *key):
        b = self.bufs.get(key)
        if b is None:
            b = self.bufs[key] = Buf(str(key))
        return b

    def _deps(self, eng, reads, writes):
        for b in reads:
            if b.w is not None and not (eng.is_pe and b.w.eng is eng):
                eng.wait(b.w)
        for b in writes:
            for t in b.r:
                if t.eng is not eng or not eng.is_pe:
                    eng.wait(t)
            if b.w is not None and (b.w.eng is not eng or not eng.is_pe):
                eng.wait(b.w)

    @staticmethod
    def _commit(tok, reads, writes):
        for b in reads:
            b.r.append(tok)
        for b in writes:
            b.w = tok
            b.r = []

    def op(self, eng, fn, reads=(), writes=()):
        self._deps(eng, reads, writes)
        eng.count += 1
        sem, tok = eng.sem, Tok(eng.sem, eng.count, eng)
        eng.ops.append(lambda e: fn(e).then_inc(sem, 1))
        self._commit(tok, reads, writes)
        return tok

    def group(self, eng, fns, reads=(), writes=()):
        self._deps(eng, reads, writes)
        eng.count += 1
        sem, tok = eng.sem, Tok(eng.sem, eng.count, eng)
        for fn in fns[:-1]:
            eng.ops.append(fn)
        last = fns[-1]
        eng.ops.append(lambda e: last(e).then_inc(sem, 1))
        self._commit(tok, reads, writes)
        return tok

    def dma(self, eng, dsem, fn, reads=(), writes=()):
        self._deps(eng, reads, writes)
        dsem.count += 16
        tok = Tok(dsem.sem, dsem.count, None)
        sem = dsem.sem
        eng.ops.append(lambda e: fn(e).then_inc(sem, 16))
        self._commit(tok, reads, writes)
        return tok

    def replay(self, block):
        def mk(eng):
            def body(e):
                for f in eng.ops:
                    f(e)
            return body
        block.tensor(mk(self.pe))
        block.scalar(mk(self.act))
        block.vector(mk(self.dve))
        block.gpsimd(mk(self.pool))
        block.sync(mk(self.sp))


class WRing:
    def __init__(self, pg, arena, dsems, plan):
        self.pg, self.arena, self.dsems, self.plan = pg, arena, dsems, plan
        self.bufs = [Buf("wslot%d" % i) for i in range(NSLOT)]
        self.next_load = 0
        self.next_get = 0
        self.released = 0
        self.pump()

    def _view(self, idx):
        _, _, shape = self.plan[idx]
        n = shape[1] * shape[2]
        return self.arena[:, idx % NSLOT, 0:n].rearrange("p (k n) -> p k n", k=shape[1])

    def pump(self):
        while self.next_load < len(self.plan) and self.next_load < self.released + NSLOT:
            idx = self.next_load
            view = self._view(idx)
            src = self.plan[idx][1]
            self.pg.dma(self.pg.sp, self.dsems[idx % NSLOT],
                        lambda e, view=view, src=src: e.dma_start(out=view, in_=src),
                        writes=[self.bufs[idx % NSLOT]])
            self.next_load += 1

    def get(self, key):
        idx = self.next_get
        assert self.plan[idx][0] == key, (self.plan[idx][0], key)
        assert idx < self.next_load, "weight tile not issued (ring too small)"
        self.next_get += 1
        return self._view(idx), self.bufs[idx % NSLOT]

    def release(self, n=1):
        self.released += n
        self.pump()


def wtile(w2d, row0, nk, col0, ncol=128):
    return w2d[row0:row0 + nk * 128, col0:col0 + ncol].rearrange("(k p) n -> p k n", p=128)


def build_program(stage=99):
    nc = bass.Bass("TRN2", target_bir_lowering=False)
    nc.dge_precook = False

    def din(name, shape, dt=F32):
        return nc.dram_tensor(name, shape, dt, kind="ExternalInput").ap()

    def dout(name, shape):
        return nc.dram_tensor(name, shape, F32, kind="ExternalOutput").ap()

    xp_d = din("xp", [2048, D])
    xs_d = din("xs", [128, D])
    sh_d = din("sh", [16, 4, 128, 128], F32R)
    sc_d = din("sc", [32, 512])
    par_d = din("par", [128, NPAR])
    cst_d = din("cst", [128, NCST])
    w1a_d = din("w1a", [D, DFF], F32R)
    w3a_d = din("w3a", [D, DFF], F32R)
    w2a_d = din("w2a", [DFF, D], F32R)
    win_d = din("win", [D, NIN], F32R)
    wab_d = din("wab", [1024, D], F32R)
    wo_d = din("wo", [D, D], F32R)
    w1b_d = din("w1b", [D, DFF], F32R)
    w3b_d = din("w3b", [D, DFF], F32R)
    w2b_d = din("w2b", [DFF, D], F32R)
    yp_d = dout("yp", [2048, D])
    ys_d = dout("ys", [128, D])
    shp_d = dout("shp", [4, 128, 128])
    scp_d = dout("scp", [2, 512])
    shs_d = dout("shs", [16, 4, 128, 128])
    scs_d = dout("scs", [32, 512])

    def ffn_plan(tag, w1, w3, w2):
        pl = []
        for g, (j0, nj) in enumerate(FFN_GROUPS):
            for j in range(j0, j0 + nj):
                pl.append(((tag, "w1", j), wtile(w1, 0, 8, j * 128), (128, 8, 128)))
                pl.append(((tag, "w3", j), wtile(w3, 0, 8, j * 128), (128, 8, 128)))
            for m in range(8):
                pl.append(((tag, "w2", g, m), wtile(w2, j0 * 128, nj, m * 128), (128, nj, 128)))
        return pl

    def mixer_plan():
        pl = []
        for kp in range(4):
            pl.append((("wv", kp), wtile(win_d, kp * 256, 2, CI, 512), (128, 2, 512)))
        if stage < 2.2:
            return pl
        for h in range(4):
            pl.append((("wq", h), wtile(win_d, 0, 8, CQ + h * 128), (128, 8, 128)))
            pl.append((("wf", h), wtile(win_d, 0, 8, CF + h * 128), (128, 8, 128)))
            pl.append((("wog", h), wtile(win_d, 0, 8, COG + h * 128), (128, 8, 128)))
        if stage < 2.3:
            return pl
        for c in range(4):
            pl.append((("wbg", c), wtile(win_d, 0, 8, CBG + c * 128), (128, 8, 128)))
            pl.append((("wcg", c), wtile(win_d, 0, 8, CCG + c * 128), (128, 8, 128)))
            pl.append((("wvv", c), wtile(win_d, 0, 8, CVV + c * 128), (128, 8, 128)))
        if stage < 2.4:
            return pl
        for qt in range(4):
            for mm in range(2):
                m = qt * 2 + mm
                pl.append((("wga", m), wtile(win_d, 0, 8, CGA + m * 128), (128, 8, 128)))
                pl.append((("wgb", m), wtile(win_d, 0, 8, CGB + m * 128), (128, 8, 128)))
                pl.append((("wab", m), wtile(wab_d, 0, 8, m * 128), (128, 8, 128)))
            for mo in range(8):
                pl.append((("wo", qt, mo), wtile(wo_d, qt * 256, 2, mo * 128), (128, 2, 128)))
        return pl

    plan = []
    for ip in range(len(PASSES)):
        if stage >= 1:
            plan += ffn_plan("a", w1a_d, w3a_d, w2a_d)
        if stage >= 2:
            plan += mixer_plan()
        if stage >= 3:
            plan += ffn_plan("b", w1b_d, w3b_d, w2b_d)

    es = ExitStack()
    with es:
        def sb(name, shape, dt):
            return es.enter_context(nc.sbuf_tensor("sb_" + name, shape, dt))

        def newsem(name):
            return es.enter_context(nc.semaphore(name))

        xT = sb("xT", [128, KD, TM], F32)
        hT = sb("hT", [128, KD, TM], F32R)
        WB = sb("WB", [128, 10, TM], F32R)
        arena = sb("arena", [128, NSLOT, 1024], F32R)
        vtok = sb("vtok", [128, 9, 512], BF16)
        NTMP = 8
        tmp = [sb("tmp%d" % i, [128, 520], F32) for i in range(NTMP)]
        xst = [sb("xst%d" % i, [128, D], F32) for i in range(2)]
        s0t = sb("s0t", [128, 8, 128], F32R)
        snw = sb("snw", [128, 8, 128], F32)
        vblk = sb("vblk", [128, 8, 128], BF16)
        S = sb("S", [128, 4, 128], F32)
        kdb = [sb("kdb%d" % i, [128, 512], BF16) for i in range(2)]
        qdb = [sb("qdb%d" % i, [128, 512], BF16) for i in range(2)]
        kkb = [sb("kkb%d" % i, [128, 512], BF16) for i in range(2)]
        cm2b = sb("cm2b", [128, 2], BF16)
        osq = sb("osq", [128, 512], F32R)
        rrow = sb("rrow", [128, 4], F32R)
        ms4 = sb("ms4", [128, 8], F32)
        identr = sb("identr", [128, 128], F32R)
        par = sb("par", [128, NPAR], F32)
        cst = sb("cst", [128, NCST], F32)
        drv = sb("drv", [128, 16], F32)
        identb = sb("identb", [128, 128], BF16)
        onesr = sb("onesr", [128, 128], F32R)
        maskP4 = sb("maskP4", [128, 512], BF16)
        maskMix = sb("maskMix", [128, 384], BF16)
        m2b = sb("m2b", [128, 16], BF16)
        neghalf = sb("neghalf", [128, 8], F32)
        ucar = sb("ucar", [128, 4, 2], F32)
        usmp = sb("usmp", [128, 4, 16, 10], F32)
        ps = es.enter_context(nc.psum_tensor("ps", [128, 8, 512], F32))
        scst = tmp[6][0:32, 0:512]
        kkT = [xst[0][:, j_ * 256:(j_ + 1) * 256].bitcast(BF16) for j_ in range(2)]
        scmb = [xst[0][:, 512 + j_ * 256:512 + (j_ + 1) * 256].bitcast(BF16) for j_ in range(2)]
        sco = tmp[7][0:32, 0:512]

        sems = {n: newsem("s_" + n) for n in ["pe", "act", "dve", "pool", "sp"]}
        pg = Prog(sems)
        B = pg.B
        wsems = [DmaSem(newsem("w%d" % i)) for i in range(NSLOT)]
        NXS = 5
        xsem = [[DmaSem(newsem("xl%d_%d" % (i, hf_))) for hf_ in range(2)] for i in range(NXS)]
        ysem = [DmaSem(newsem("ys%d" % i)) for i in range(2)]
        msem = DmaSem(newsem("misc"))
        s0sem = DmaSem(newsem("s0l"))
        snsem = DmaSem(newsem("sns"))
        s0sem2 = [DmaSem(newsem("s0l%d" % i)) for i in range(2)]
        snsem2 = [DmaSem(newsem("sns%d" % i)) for i in range(2)]
        osem = DmaSem(newsem("outs"))
        block = es.enter_context(nc.Block())

        pe, act, dve, pool, sp = pg.pe, pg.act, pg.dve, pg.pool, pg.sp
        PSB = [B("ps", i) for i in range(8)]
        TB = [B("tmp", i) for i in range(NTMP)]
        ident = cst[:, C_ID:C_ID + 128]
        CST = B("cst")

        def pcol(c):
            return par[:, c:c + 1]

        pg.dma(sp, msem, lambda e: e.dma_start(out=par[:], in_=par_d), writes=[B("par")])
        pg.dma(sp, msem, lambda e: e.dma_start(out=cst[:], in_=cst_d), writes=[CST])
        t_last = pg.dma(sp, msem, lambda e: e.dma_start(out=scst, in_=sc_d), writes=[TB[6]])
        for b_ in [B("par"), CST, TB[6]]:
            b_.w = t_last
        W = None

        pg.op(pool, lambda e: e.memset(neghalf[:], -0.5), writes=[B("neghalf")])
        pg.op(pool, lambda e: e.memset(S[:], 0.0), writes=[B("S")])
        pg.op(pool, lambda e: e.memset(ucar[:], 0.0), writes=[B("ucar")])
        pg.op(dve, lambda e: e.tensor_copy(identb[:], ident), reads=[CST], writes=[B("identb")])
        pg.op(dve, lambda e: e.tensor_copy(onesr[:], cst[:, C_ONE:C_ONE + 128]), reads=[CST], writes=[B("onesr")])
        pg.op(dve, lambda e: e.tensor_copy(identr[:], ident), reads=[CST], writes=[B("identr")])
        for i_ in range(4):
            pg.op(dve, lambda e, i_=i_: e.tensor_copy(maskP4[:, i_ * 128:(i_ + 1) * 128], cst[:, C_MP:C_MP + 128]), reads=[CST], writes=[B("masks")])
        for i_ in range(3):
            c_ = C_MP if i_ < 2 else C_MS
            pg.op(dve, lambda e, i_=i_, c_=c_: e.tensor_copy(maskMix[:, i_ * 128:(i_ + 1) * 128], cst[:, c_:c_ + 128]), reads=[CST], writes=[B("masks")])
        pg.op(dve, lambda e: e.tensor_copy(m2b[:], cst[:, C_M2:C_M2 + 16]), reads=[CST], writes=[B("m2b")])
        pg.op(dve, lambda e: e.tensor_copy(cm2b[:], cst[:, C_CM2:C_CM2 + 2]), reads=[CST], writes=[B("cm2b")])
        pg.op(dve, lambda e: e.tensor_tensor(out=drv[:, 8:12], in0=par[:, PL0:PL0 + 4], in1=par[:, PL1:PL1 + 4], op=ALU.subtract),
              reads=[B("par")], writes=[B("drv")])
        pg.op(act, lambda e: e.activation(out=drv[:, 12:16], in_=drv[:, 8:12], func=AF.Tanh, scale=0.5),
              reads=[B("drv")], writes=[B("drv")])
        pg.op(dve, lambda e: e.tensor_scalar(out=drv[:, 0:4], in0=drv[:, 12:16], scalar1=-0.25, scalar2=0.25, op0=ALU.mult, op1=ALU.add),
              reads=[B("drv")], writes=[B("drv")])
        pg.op(dve, lambda e: e.tensor_scalar(out=drv[:, 4:8], in0=drv[:, 0:4], scalar1=-1.0, scalar2=None, op0=ALU.mult),
              reads=[B("drv")], writes=[B("drv")])
        pg.op(dve, lambda e: e.tensor_scalar(out=drv[:, 8:12], in0=drv[:, 0:4], scalar1=-1.0, scalar2=1.0, op0=ALU.mult, op1=ALU.add),
              reads=[B("drv")], writes=[B("drv")])
        DRV = B("drv")

        def load_sample_conv_state():
            fns = []
            for c in range(4):
                fns.append(lambda e, c=c: e.transpose(ps[:, 7, c * 32:(c + 1) * 32], scst[:, c * 128:(c + 1) * 128], cst[0:32, C_ID:C_ID + 32]))
            pg.group(pe, fns, reads=[TB[6], CST], writes=[PSB[7]])
            for c in range(4):
                pg.op(dve, lambda e, c=c: e.tensor_copy(usmp[:, c, :, 0:2], ps[:, 7, c * 32:(c + 1) * 32].rearrange("p (i t) -> p i t", t=2)),
                      reads=[PSB[7]], writes=[B("usmp", c)])

        def xslot(sl, hf):
            if sl < 2:
                return xst[sl][:, hf * 512:(hf + 1) * 512], B("xst", sl, hf)
            a_ = 2 * (sl - 2) + hf
            return tmp[a_][:, 0:512], TB[a_]

        def load_x(ip):
            pa = PASSES[ip]
            nblk = pa["T"] // 128
            for b in range(nblk):
                sl = b % NXS
                for hf in range(2):
                    ap_, buf_ = xslot(sl, hf)
                    if b < pa["npb"]:
                        r0 = pa["ptok0"] + b * 128
                        src = xp_d[r0:r0 + 128, hf * 512:(hf + 1) * 512]
                    else:
                        src = xs_d[:, hf * 512:(hf + 1) * 512]
                    pg.dma(sp, xsem[sl][hf], lambda e, ap_=ap_, src=src: e.dma_start(out=ap_, in_=src), writes=[buf_])
                for hf in range(2):
                    ap_, buf_ = xslot(sl, hf)
                    bank = 4 + 2 * (b % 2) + hf
                    fns = [lambda e, k=k, ap_=ap_, bank=bank: e.transpose(ps[:, bank, (k % 4) * 128:(k % 4 + 1) * 128],
                                                                          ap_[:, (k % 4) * 128:(k % 4 + 1) * 128], ident)
                           for k in range(hf * 4, hf * 4 + 4)]
                    pg.group(pe, fns, reads=[buf_, CST], writes=[PSB[bank]])
                    dst = xT[:, hf * 4:hf * 4 + 4, b * 128:(b + 1) * 128]
                    src_ps = ps[:, bank, :].rearrange("p (k n) -> p k n", k=4)
                    wr = [B("xT", k, b) for k in range(hf * 4, hf * 4 + 4)]
                    if hf == 0:
                        pg.op(act, lambda e, dst=dst, src_ps=src_ps: e.activation(out=dst, in_=src_ps, func=AF.Copy),
                              reads=[PSB[bank]], writes=wr)
                    else:
                        pg.op(dve, lambda e, dst=dst, src_ps=src_ps: e.tensor_copy(dst, src_ps),
                              reads=[PSB[bank]], writes=wr)

        def mm(out, lhsT, rhs, start, stop, skip=False):
            return lambda e: e.matmul(out, lhsT=lhsT, rhs=rhs, start=start, stop=stop, skip_group_check=skip)

        def proj(wv_, wb_, bank, si, c0, n):
            pg.group(pe, [mm(ps[:, bank, 0:n], wv_[:, k, :], hT[:, k, c0:c0 + n], k == 0, k == KD - 1) for k in range(KD)],
                     reads=[wb_, B("hT", si)], writes=[PSB[bank]])

        def rstd_bcast(sqs, sqbufs, n, inv_dim, ss_bank, out_bank, split=False):
            nb = n // 128
            nk = len(sqs)
            if nk > 1:
                pg.group(pe, [mm(ps[:, ss_bank, 0:n], onesr[:], sqs[k], k == 0, k == nk - 1) for k in range(nk)],
                         reads=list(sqbufs) + [B("onesr")], writes=[PSB[ss_bank]])
                pg.op(act, lambda e: e.activation(out=osq[0:1, 0:n], in_=ps[0:1, ss_bank, 0:n], func=AF.Copy),
                      reads=[PSB[ss_bank]], writes=[B("osq")])
                pg.group(pe, [mm(ps[:, ss_bank, 2 * blk:2 * blk + 2], osq[0:1, blk * 128:(blk + 1) * 128], onesr[0:1, 0:2], True, True, True)
                              for blk in range(nb)], reads=[B("osq"), B("onesr")], writes=[PSB[ss_bank]])
            else:
                fns = []
                for blk in range(nb):
                    for k in range(nk):
                        fns.append(mm(ps[:, ss_bank, 2 * blk:2 * blk + 2], sqs[k][:, blk * 128:(blk + 1) * 128], onesr[:, 0:2], k == 0, k == nk - 1, True))
                pg.group(pe, fns, reads=list(sqbufs) + [B("onesr")], writes=[PSB[ss_bank]])
            pg.op(dve, lambda e: e.tensor_scalar(out=ms4[:, 0:nb], in0=ps[:, ss_bank, 0:2 * nb].rearrange("p (b t) -> p b t", t=2)[:, :, 0],
                                                  scalar1=inv_dim, scalar2=EPS, op0=ALU.mult, op1=ALU.add), reads=[PSB[ss_bank]], writes=[B("ms4")])
            pg.op(pool, lambda e: e.tensor_tensor(out=rrow[:, 0:nb], in0=ms4[:, 0:nb], in1=neghalf[:, 0:nb], op=ALU.pow),
                  reads=[B("ms4"), B("neghalf")], writes=[B("rrow")])
            if not split:
                rstd_b(n, out_bank)

        def rstd_b(n, out_bank):
            nb = n // 128
            pg.group(pe, [mm(ps[:, out_bank, blk * 128:(blk + 1) * 128], rrow[:, blk:blk + 1].broadcast_to([128, 128]), identr[:], True, True, True)
                          for blk in range(nb)],
                     reads=[B("rrow"), B("identr")], writes=[PSB[out_bank]])

        def norm(ip, gcol, dst_is_h=True):
            pa = PASSES[ip]
            for si, (c0, n) in enumerate(pa["subs"]):
                blks = range(c0 // 128, (c0 + n) // 128)
                for k in range(KD):
                    pg.op(act, lambda e, k=k, c0=c0, n=n: e.activation(out=WB[:, k, c0:c0 + n], in_=xT[:, k, c0:c0 + n], func=AF.Square),
                          reads=[B("xT", k, b) for b in blks], writes=[B("WB", k, si)])
                rstd_bcast([WB[:, k, c0:c0 + n] for k in range(KD)], [B("WB", k, si) for k in range(KD)], n, 1.0 / D, 6, 7)
                for k in range(KD):
                    pg.op(dve, lambda e, k=k, c0=c0, n=n: e.scalar_tensor_tensor(out=hT[:, k, c0:c0 + n], in0=xT[:, k, c0:c0 + n],
                                                                                 scalar=pcol(gcol + k), in1=ps[:, 7, 0:n],
                                                                                 op0=ALU.mult, op1=ALU.mult),
                          reads=[B("xT", k, b) for b in blks] + [PSB[7], B("par")], writes=[B("hT", si)])

        def xupdate(ip, m, si, c0, n, bank, scale):
            blks = range(c0 // 128, (c0 + n) // 128)
            xb = [B("xT", m, b) for b in blks]
            pg.op(dve, lambda e: e.scalar_tensor_tensor(out=xT[:, m, c0:c0 + n], in0=ps[:, bank, 0:n], scalar=scale,
                                                        in1=xT[:, m, c0:c0 + n], op0=ALU.mult, op1=ALU.add),
                  reads=[PSB[bank]] + xb, writes=xb)

        def ffn(ip, tag):
            pa = PASSES[ip]
            it = 0
            ity = 0
            for g, (j0, nj) in enumerate(FFN_GROUPS):
                for jj in range(nj):
                    j = j0 + jj
                    w1v, w1b = W.get((tag, "w1", j))
                    w3v, w3b = W.get((tag, "w3", j))
                    for si, (c0, n) in enumerate(pa["subs"]):
                        b1, b3 = it % 2, 2 + it % 2
                        st = tmp[it % 2]
                        it += 1
                        pg.group(pe, [lambda e, k=k, w1v=w1v, b1=b1, c0=c0, n=n: e.matmul(ps[:, b1, 0:n], lhsT=w1v[:, k, :], rhs=hT[:, k, c0:c0 + n],
                                                                                     start=(k == 0), stop=(k == KD - 1)) for k in range(KD)],
                                 reads=[w1b, B("hT", si)], writes=[PSB[b1]])
                        pg.group(pe, [lambda e, k=k, w3v=w3v, b3=b3, c0=c0, n=n: e.matmul(ps[:, b3, 0:n], lhsT=w3v[:, k, :], rhs=hT[:, k, c0:c0 + n],
                                                                                     start=(k == 0), stop=(k == KD - 1)) for k in range(KD)],
                                 reads=[w3b, B("hT", si)], writes=[PSB[b3]])
                        pg.op(act, lambda e, st=st, b1=b1, n=n: e.activation(out=st[:, 0:n], in_=ps[:, b1, 0:n], func=AF.Silu),
                              reads=[PSB[b1]], writes=[TB[(it - 1) % 2]])
                        pg.op(dve, lambda e, st=st, b3=b3, jj=jj, c0=c0, n=n: e.tensor_tensor(out=WB[:, jj, c0:c0 + n], in0=st[:, 0:n],
                                                                                             in1=ps[:, b3, 0:n], op=ALU.mult),
                              reads=[TB[(it - 1) % 2], PSB[b3]], writes=[B("WB", jj, si)])
                    W.release(2)
                for m in range(8):
                    w2v, w2b = W.get((tag, "w2", g, m))
                    for si, (c0, n) in enumerate(pa["subs"]):
                        by = 4 + ity % 4
                        ity += 1
                        pg.group(pe, [lambda e, jj=jj, w2v=w2v, by=by, c0=c0, n=n, nj=nj: e.matmul(ps[:, by, 0:n], lhsT=w2v[:, jj, :], rhs=WB[:, jj, c0:c0 + n],
                                                                                            start=(jj == 0), stop=(jj == nj - 1)) for jj in range(nj)],
                                 reads=[w2b] + [B("WB", jj, si) for jj in range(nj)], writes=[PSB[by]])
                        xupdate(ip, m, si, c0, n, by, 0.5)
                    W.release(1)

        def final(ip):
            pa = PASSES[ip]
            for si, (c0, n) in enumerate(pa["subs"]):
                blks = range(c0 // 128, (c0 + n) // 128)
                for k in range(KD):
                    pg.op(act, lambda e, k=k, c0=c0, n=n: e.activation(out=WB[:, k, c0:c0 + n], in_=xT[:, k, c0:c0 + n], func=AF.Square),
                          reads=[B("xT", k, b) for b in blks], writes=[B("WB", k, si)])
                rstd_bcast([WB[:, k, c0:c0 + n] for k in range(KD)], [B("WB", k, si) for k in range(KD)], n, 1.0 / D, 5, 4)
                for k in range(KD):
                    pg.op(dve, lambda e, k=k, c0=c0, n=n: e.scalar_tensor_tensor(out=tmp[k][:, 0:n], in0=xT[:, k, c0:c0 + n],
                                                                                 scalar=pcol(PGF + k), in1=ps[:, 4, 0:n],
                                                                                 op0=ALU.mult, op1=ALU.mult),
                          reads=[B("xT", k, b) for b in blks] + [PSB[4], B("par")], writes=[TB[k]])
                for b in blks:
                    bc = b * 128
                    st = xst[b % 2]
                    STB = [B("xst", b % 2, 0), B("xst", b % 2, 1)]
                    for hf in range(2):
                        bank = [6, 7, 2, 3][(b % 2) * 2 + hf]
                        pg.group(pe, [lambda e, k=k, bank=bank, lc=bc - c0: e.transpose(ps[:, bank, (k % 4) * 128:(k % 4 + 1) * 128],
                                                                                         tmp[k][:, lc:lc + 128], ident)
                                      for k in range(hf * 4, hf * 4 + 4)], reads=[TB[k] for k in range(hf * 4, hf * 4 + 4)] + [CST], writes=[PSB[bank]])
                        if hf == 0:
                            pg.op(act, lambda e, st=st, bank=bank: e.activation(out=st[:, 0:512], in_=ps[:, bank, :], func=AF.Copy),
                                  reads=[PSB[bank]], writes=[STB[0]])
                        else:
                            pg.op(dve, lambda e, st=st, bank=bank: e.tensor_copy(st[:, 512:1024], ps[:, bank, :]),
                                  reads=[PSB[bank]], writes=[STB[1]])
                    if b < pa["npb"]:
                        r0 = pa["ptok0"] + b * 128
                        dst = yp_d[r0:r0 + 128, :]
                    else:
                        dst = ys_d[:, :]
                    pg.dma(sp, ysem[b % 2], lambda e, st=st, dst=dst: e.dma_start(out=dst, in_=st[:]), reads=STB)

        def mixer(ip):
            pa = PASSES[ip]
            T, subs, npb, sample = pa["T"], pa["subs"], pa["npb"], pa["sample"]
            nblk = T // 128
            last_pass = (ip == len(PASSES) - 1)
            norm(ip, PGM)

            def sub_of(b):
                for si_, (c0_, n_) in enumerate(subs):
                    if c0_ <= b * 128 < c0_ + n_:
                        return si_

            wv = [W.get(("wv", kp)) for kp in range(4)]
            for b in range(nblk):
                bank = b % 2
                pg.group(pe, [mm(ps[:, bank, :], hT[:, k, b * 128:(b + 1) * 128], wv[k // 2][0][:, k % 2, :], k == 0, k == KD - 1)
                              for k in range(KD)],
                         reads=[B("hT", sub_of(b))] + [x[1] for x in wv], writes=[PSB[bank]])
                if b % 2 == 0:
                    pg.op(act, lambda e, b=b, bank=bank: e.activation(out=vtok[:, b, :], in_=ps[:, bank, :], func=AF.Copy),
                          reads=[PSB[bank]], writes=[B("vtok", b)])
                else:
                    pg.op(dve, lambda e, b=b, bank=bank: e.tensor_copy(vtok[:, b, :], ps[:, bank, :]),
                          reads=[PSB[bank]], writes=[B("vtok", b)])
            W.release(4)

            if stage < 2.2:
                return
            psb1 = ps[:, 1, :].bitcast(BF16)
            NRING = 24

            def SH(e_):
                e_ = e_ % NRING
                return WB[:, 5 + e_ // 8, (e_ % 8) * 128:(e_ % 8 + 1) * 128]

            st = dict(bi=0, ring=0, ap=0, a3=0)
            items = [(h, si) for h in range(4) for si in range(len(subs))]
            wts = {}
            info = {}

            def stage_A(h, si):
                c0, n = subs[si]
                if si == 0:
                    wts[h] = (W.get(("wq", h)), W.get(("wf", h)), W.get(("wog", h)))
                    e0 = st["ring"]
                    st["ring"] += 1
                    pg.op(dve, lambda e, h=h, e0=e0: e.tensor_copy(SH(e0), S[:, h, :]), reads=[B("S", h), B("S")], writes=[B("SH", e0 % NRING)])
                    info[("cur", h)] = e0
                (wq, wqb), (wf, wfb), (wg, wgb) = wts[h]
                ap = st["ap"] % 2
                st["ap"] += 1
                has_s = sample and (c0 + n > npb * 128)
                npc = (npb * 128 - c0) if has_s else n
                Pt, PB_ = (tmp[6], TB[6]) if ap == 0 else (tmp[7], TB[7])
                a3 = st["a3"] % 3
                st["a3"] += 1
                sg, SGB = [(tmp[1], TB[1]), (tmp[2], TB[2]), (tmp[3], TB[3])][a3]
                kd_, KDB = kdb[ap], B("kdb", ap)
                qd_, QDB = qdb[ap], B("qdb", ap)
                kk_, KKB = kkb[ap], B("kkb", ap)
                qr_, QRB = WB[:, [8, 9, 4][a3], 0:512], B("qdR", a3)
                info[(h, si)] = dict(ap=ap, Pt=Pt, PB_=PB_, sg=sg, SGB=SGB, kd_=kd_, KDB=KDB, qd_=qd_, QDB=QDB, kk_=kk_, KKB=KKB,
                                     qr_=qr_, QRB=QRB, has_s=has_s, npc=npc, obank=3 + ap)
                proj(wf, wfb, 1, si, c0, n)
                pg.op(act, lambda e: e.activation(out=tmp[0][:, 0:n], in_=ps[:, 1, 0:n], func=AF.Tanh, scale=0.5),
                      reads=[PSB[1]], writes=[TB[0]])
                pg.op(act, lambda e: e.activation(out=tmp[4][:, 0:n], in_=tmp[0][:, 0:n], func=AF.Identity, scale=drv[:, 4 + h:5 + h],
                                                  bias=drv[:, h:h + 1]), reads=[TB[0], DRV], writes=[TB[4]])
                pg.op(act, lambda e: e.activation(out=tmp[5][:, 0:n], in_=tmp[0][:, 0:n], func=AF.Identity, scale=drv[:, h:h + 1],
                                                  bias=drv[:, 8 + h:9 + h]), reads=[TB[0], DRV], writes=[TB[5]])
                yield
                proj(wq, wqb, 0, si, c0, n)
                yield
                proj(wg, wgb, 2, si, c0, n)
                if si == len(subs) - 1:
                    W.release(3)
                pg.op(act, lambda e: e.activation(out=sg[:, 0:n], in_=ps[:, 2, 0:n], func=AF.Silu),
                      reads=[PSB[2]], writes=[SGB])
                yield
                rm0 = C_RMIX if has_s else C_R64
                pg.op(dve, lambda e: e.tensor_tensor_scan(out=Pt[:, 0:n], data0=cst[:, rm0:rm0 + n], data1=tmp[5][:, 0:n],
                                                          initial=1.0, op0=ALU.max, op1=ALU.mult),
                      reads=[TB[5], CST], writes=[PB_])
                yield
                if False:
                    pg.op(pool, lambda e: e.tensor_tensor(out=kd_[:, 0:n], in0=tmp[4][:, 0:n], in1=Pt[:, 0:n], op=ALU.divide),
                          reads=[TB[4], PB_], writes=[KDB])
                else:
                    pg.op(dve, lambda e: e.reciprocal(tmp[5][:, 0:n], Pt[:, 0:n]), reads=[PB_], writes=[TB[5]])
                    pg.op(dve, lambda e: e.tensor_tensor(out=kd_[:, 0:n], in0=tmp[4][:, 0:n], in1=tmp[5][:, 0:n], op=ALU.mult),
                          reads=[TB[4], TB[5]], writes=[KDB])
                yield
                pg.op(dve, lambda e: e.tensor_tensor(out=qr_[:, 0:n], in0=ps[:, 0, 0:n], in1=Pt[:, 0:n], op=ALU.mult),
                      reads=[PSB[0], PB_], writes=[QRB])
                pg.op(dve, lambda e: e.tensor_copy(qd_[:, 0:n], qr_[:, 0:n].bitcast(F32)), reads=[QRB], writes=[QDB])
                yield
                nch = npc // 64
                pg.op(dve, lambda e: e.tensor_tensor(out=kk_[:, 0:npc].rearrange("p (c t) -> p c t", t=64),
                                                      in0=kd_[:, 0:npc].rearrange("p (c t) -> p c t", t=64),
                                                      in1=Pt[:, 0:npc].rearrange("p (c t) -> p c t", t=64)[:, :, 63:64].broadcast_to([128, nch, 64]),
                                                      op=ALU.mult), reads=[KDB, PB_], writes=[KKB])
                if has_s:
                    pg.op(dve, lambda e: e.tensor_tensor(out=kk_[:, npc:n].rearrange("p (c t) -> p c t", t=8),
                                                          in0=kd_[:, npc:n].rearrange("p (c t) -> p c t", t=8),
                                                          in1=Pt[:, npc:n].rearrange("p (c t) -> p c t", t=8)[:, :, 7:8].broadcast_to([128, 16, 8]),
                                                          op=ALU.mult), reads=[KDB, PB_], writes=[KKB])

            def stage_B1(h, si):
                c0, n = subs[si]
                I = info[(h, si)]
                Pt, PB_, kd_, KDB, qd_, QDB, kk_, KKB, qr_, QRB, obank = (I["Pt"], I["PB_"], I["kd_"], I["KDB"], I["qd_"], I["QDB"],
                                                                           I["kk_"], I["KKB"], I["qr_"], I["QRB"], I["obank"])
                I["chunks"] = []
                blks = list(range(c0 // 128, (c0 + n) // 128))
                nb = len(blks)
                has_s = I["has_s"]
                npb_ = nb - 1 if has_s else nb
                st["bi"] += 1
                j = st["bi"] % 2
                kt, KTB = kkT[j], B("kkT", j)
                sm, SMB = scmb[j], B("scmb", j)
                b0 = blks[0]
                hc = slice(h * 128, (h + 1) * 128)
                if has_s:
                    src0 = sh_d[0:4, h].rearrange("s k v -> k s v")
                    pg.dma(sp, s0sem2[0], lambda e, src0=src0: e.dma_start(out=s0t[:, 0:4, :], in_=src0), writes=[B("s0t", 0)])
                pg.op(pool, lambda e: e.tensor_tensor(out=vblk[:, 0:2 * npb_, :].rearrange("p (b c) v -> p b c v", c=2),
                                                      in0=vtok[:, b0:b0 + npb_, hc].unsqueeze(2).broadcast_to([128, npb_, 2, 128]),
                                                      in1=cm2b[:, 0:2].unsqueeze(1).unsqueeze(3).broadcast_to([128, npb_, 2, 128]), op=ALU.mult),
                      reads=[B("vtok", b) for b in blks[:npb_]] + [B("cm2b")], writes=[B("vblk")])
                yield
                pg.group(pe, [lambda e, ib=ib: e.transpose(psb1[:, ib * 128:(ib + 1) * 128], kk_[:, ib * 128:(ib + 1) * 128], identb[:]) for ib in range(nb)],
                         reads=[KKB, B("identb")], writes=[PSB[1]])
                pg.op(act, lambda e: e.activation(out=kt[:, 0:nb * 128], in_=psb1[:, 0:nb * 128], func=AF.Copy), reads=[PSB[1]], writes=[KTB])
                yield
                pg.group(pe, [mm(ps[:, 5, ib * 128:(ib + 1) * 128], kd_[:, ib * 128:(ib + 1) * 128], qd_[:, ib * 128:(ib + 1) * 128], True, True, True)
                              for ib in range(nb)], reads=[KDB, QDB], writes=[PSB[5]])
                mt = maskMix if has_s else maskP4
                pg.op(dve, lambda e: e.tensor_tensor(out=sm[:, 0:nb * 128], in0=ps[:, 5, 0:nb * 128], in1=mt[:, 0:nb * 128], op=ALU.mult),
                      reads=[PSB[5], B("masks")], writes=[SMB])
                yield
                dbufs = [PSB[6]] + ([PSB[7]] if npb_ > 2 else [])
                pg.group(pe, [mm(ps[:, 6 + ib // 2, (ib % 2) * 256:(ib % 2) * 256 + 256], kt[:, ib * 128:(ib + 1) * 128], vblk[:, 2 * ib:2 * ib + 2, :], True, True, True)
                              for ib in range(npb_)], reads=[KTB, B("vblk")], writes=dbufs)
                pg.group(pe, [mm(ps[:, obank, ib * 128:(ib + 1) * 128], vtok[:, blks[ib], hc], sm[:, ib * 128:(ib + 1) * 128], ib == 0, False, True)
                              for ib in range(nb)], reads=[B("vtok", b) for b in blks] + [SMB], writes=[PSB[obank]])
                yield
                for ib in range(npb_):
                    lc = ib * 128
                    dbank = 6 + ib // 2
                    dcol = (ib % 2) * 256
                    for c2 in range(2):
                        cc = lc + c2 * 64
                        eprev = info[("cur", h)]
                        enew = st["ring"]
                        st["ring"] += 1
                        pg.op(dve, lambda e, eprev=eprev, enew=enew, cc=cc, dbank=dbank, dcol=dcol, c2=c2: e.scalar_tensor_tensor(
                            out=SH(enew), in0=SH(eprev).bitcast(F32), scalar=Pt[:, cc + 63:cc + 64],
                            in1=ps[:, dbank, dcol + c2 * 128:dcol + (c2 + 1) * 128], op0=ALU.mult, op1=ALU.add),
                            reads=[B("SH", eprev % NRING), PB_, PSB[dbank]], writes=[B("SH", enew % NRING)])
                        I["chunks"].append((cc, eprev, c2 == 1))
                        info[("cur", h)] = enew
                    yield
                if has_s:
                    ib = nb - 1
                    lc = ib * 128
                    b = blks[ib]
                    vsl = vtok[:, b, hc]
                    ktS = kt[:, ib * 128:(ib + 1) * 128]
                    for r4 in range(4):
                        jb = r4 % 2
                        if r4 + 1 < 4:
                            srcn = sh_d[(r4 + 1) * 4:(r4 + 2) * 4, h].rearrange("s k v -> k s v")
                            jn = (r4 + 1) % 2
                            pg.dma(sp, s0sem2[jn], lambda e, srcn=srcn, jn=jn: e.dma_start(out=s0t[:, jn * 4:jn * 4 + 4, :], in_=srcn),
                                   writes=[B("s0t", jn)])
                        pg.group(pe, [mm(ps[:, obank, lc + (r4 * 4 + i) * 8:lc + (r4 * 4 + i) * 8 + 8], s0t[:, jb * 4 + i, :],
                                         qr_[:, lc + (r4 * 4 + i) * 8:lc + (r4 * 4 + i) * 8 + 8], False, (r4 == 3 and i == 3), True) for i in range(4)],
                                 reads=[B("s0t", jb), QRB], writes=[PSB[obank]])
                        pg.op(pool, lambda e, r4=r4: e.tensor_tensor(
                            out=vblk[:, 4:8, :], in0=vsl.unsqueeze(1).broadcast_to([128, 4, 128]),
                            in1=m2b[:, r4 * 4:r4 * 4 + 4].unsqueeze(2).broadcast_to([128, 4, 128]), op=ALU.mult),
                            reads=[B("vtok", b), B("m2b")], writes=[B("vblk")])
                        pg.group(pe, [mm(ps[:, 7, :], ktS, vblk[:, 4:8, :], True, True, True)], reads=[KTB, B("vblk")], writes=[PSB[7]])
                        for i4 in range(4):
                            pl = lc + (r4 * 4 + i4) * 8 + 7
                            pg.op(dve, lambda e, jb=jb, i4=i4, pl=pl: e.scalar_tensor_tensor(
                                out=snw[:, jb * 4 + i4, :], in0=s0t[:, jb * 4 + i4, :].bitcast(F32), scalar=Pt[:, pl:pl + 1],
                                in1=ps[:, 7, i4 * 128:(i4 + 1) * 128], op0=ALU.mult, op1=ALU.add),
                                reads=[B("s0t", jb), PB_, PSB[7]], writes=[B("snw", jb)])
                        dst = shs_d[r4 * 4:(r4 + 1) * 4, h].rearrange("s k v -> k s v")
                        pg.dma(sp, snsem2[jb], lambda e, dst=dst, jb=jb: e.dma_start(out=dst, in_=snw[:, jb * 4:jb * 4 + 4, :]), reads=[B("snw", jb)])
                        yield
                if si == len(subs) - 1:
                    ecur = info[("cur", h)]
                    pg.op(dve, lambda e, ecur=ecur: e.tensor_copy(S[:, h, :], SH(ecur).bitcast(F32)), reads=[B("SH", ecur % NRING)], writes=[B("S", h)])

            def stage_B2G(h, si):
                c0, n = subs[si]
                I = info[(h, si)]
                obank, qr_, QRB, sg, SGB = I["obank"], I["qr_"], I["QRB"], I["sg"], I["SGB"]
                for (cc, eprev, last) in I["chunks"]:
                    pg.group(pe, [mm(ps[:, obank, cc:cc + 64], SH(eprev), qr_[:, cc:cc + 64], False, last, True)],
                             reads=[B("SH", eprev % NRING), QRB], writes=[PSB[obank]])
                yield
                pg.op(act, lambda e: e.activation(out=osq[:, 0:n], in_=ps[:, obank, 0:n], func=AF.Square), reads=[PSB[obank]], writes=[B("osq")])
                yield
                rstd_bcast([osq[:, 0:n]], [B("osq")], n, 1.0 / 128, 2, 2, split=True)
                yield
                yield
                rstd_b(n, 2)
                pg.op(dve, lambda e: e.tensor_tensor(out=sg[:, 0:n], in0=sg[:, 0:n], in1=ps[:, 2, 0:n], op=ALU.mult),
                      reads=[SGB, PSB[2]], writes=[SGB])
                pg.op(dve, lambda e: e.scalar_tensor_tensor(out=WB[:, h, c0:c0 + n], in0=ps[:, obank, 0:n], scalar=pcol(PGH + h),
                                                            in1=sg[:, 0:n], op0=ALU.mult, op1=ALU.mult),
                      reads=[PSB[obank], SGB, B("par")], writes=[B("WB", h, si)])

            def drive(gens):
                gens = [g for g in gens if g is not None]
                while gens:
                    for g in list(gens):
                        try:
                            next(g)
                        except StopIteration:
                            gens.remove(g)

            drive([stage_A(*items[0])])
            for idx in range(len(items)):
                gB2G = stage_B2G(*items[idx - 1]) if idx >= 1 else iter(())
                gB1 = stage_B1(*items[idx])
                gA = stage_A(*items[idx + 1]) if idx + 1 < len(items) else iter(())

                def step(g_, k_=1):
                    for _ in range(k_):
                        try:
                            next(g_)
                        except StopIteration:
                            return
                for g_ in (gA, gB2G, gB1, gB1, gB2G, gA, gB1, gB2G, gA, gA, gA, gB1):
                    step(g_)
                drive([gB1])
                drive([gA])
                drive([gB2G])
            drive([stage_B2G(*items[-1])])
            if last_pass:
                pg.dma(sp, osem, lambda e: e.dma_start(out=shp_d.rearrange("h k v -> k h v"), in_=S[:]),
                       reads=[B("S", h) for h in range(4)])

            if stage < 2.3:
                return
            itc = 0
            for c in range(4):
                wbg, wbgb = W.get(("wbg", c))
                wcg, wcgb = W.get(("wcg", c))
                wvv, wvvb = W.get(("wvv", c))
                w0, w1, w2 = pcol(PCW + c * 3 + 0), pcol(PCW + c * 3 + 1), pcol(PCW + c * 3 + 2)
                for si, (c0, n) in enumerate(subs):
                    itc += 1
                    bA, bB, bC = itc % 2, 2 + itc % 2, 4 + itc % 2
                    proj(wbg, wbgb, bA, si, c0, n)
                    proj(wcg, wcgb, bB, si, c0, n)
                    proj(wvv, wvvb, bC, si, c0, n)
                    npc = (min(c0 + n, npb * 128) - c0) if sample else n
                    has_s = n > npc
                    vvs, VVB = tmp[itc % 2], TB[itc % 2]
                    ut, UTB = tmp[4 + itc % 2], TB[4 + itc % 2]
                    pg.op(act, lambda e, vvs=vvs, bC=bC, n=n: e.activation(out=vvs[:, 0:n], in_=ps[:, bC, 0:n], func=AF.Copy),
                          reads=[PSB[bC]], writes=[VVB])
                    pg.op(pool, lambda e, ut=ut, c=c: e.tensor_copy(ut[:, 0:2], ucar[:, c, :]), reads=[B("ucar", c), B("ucar")], writes=[UTB])
                    pg.op(dve, lambda e, ut=ut, vvs=vvs, bB=bB, npc=npc: e.tensor_tensor(out=ut[:, 2:2 + npc], in0=ps[:, bB, 0:npc], in1=vvs[:, 0:npc], op=ALU.mult),
                          reads=[PSB[bB], VVB], writes=[UTB])
                    pg.op(pool, lambda e, ut=ut, c=c, npc=npc: e.tensor_copy(ucar[:, c, :], ut[:, npc:npc + 2]), reads=[UTB], writes=[B("ucar", c)])
                    pg.op(dve, lambda e, ut=ut, npc=npc, w2=w2: e.tensor_scalar(out=tmp[6][:, 0:npc], in0=ut[:, 2:2 + npc], scalar1=w2, scalar2=None, op0=ALU.mult),
                          reads=[UTB, B("par")], writes=[TB[6]])
                    pg.op(dve, lambda e, ut=ut, npc=npc, w1=w1: e.scalar_tensor_tensor(out=tmp[6][:, 0:npc], in0=ut[:, 1:1 + npc], scalar=w1, in1=tmp[6][:, 0:npc],
                                                                                      op0=ALU.mult, op1=ALU.add), reads=[UTB, TB[6]], writes=[TB[6]])
                    pg.op(dve, lambda e, ut=ut, npc=npc, w0=w0: e.scalar_tensor_tensor(out=tmp[6][:, 0:npc], in0=ut[:, 0:npc], scalar=w0, in1=tmp[6][:, 0:npc],
                                                                                      op0=ALU.mult, op1=ALU.add), reads=[UTB, TB[6]], writes=[TB[6]])
                    pg.op(dve, lambda e, bA=bA, npc=npc, c=c, c0=c0: e.tensor_tensor(out=WB[:, 4 + c, c0:c0 + npc], in0=ps[:, bA, 0:npc], in1=tmp[6][:, 0:npc], op=ALU.mult),
                          reads=[PSB[bA], TB[6]], writes=[B("WB", 4 + c, si)])
                    if has_s:
                        def v3(ap):
                            return ap.rearrange("p (i t) -> p i t", t=8)
                        us = usmp[:, c]
                        USB = B("usmp", c)
                        cvs = v3(tmp[7][:, 0:128])
                        pg.op(dve, lambda e, us=us, vvs=vvs, bB=bB, npc=npc, n=n: e.tensor_tensor(out=us[:, :, 2:10], in0=v3(ps[:, bB, npc:n]), in1=v3(vvs[:, npc:n]), op=ALU.mult),
                              reads=[PSB[bB], VVB, USB], writes=[USB])
                        pg.op(dve, lambda e, us=us, cvs=cvs, w2=w2: e.tensor_scalar(out=cvs, in0=us[:, :, 2:10], scalar1=w2, scalar2=None, op0=ALU.mult),
                              reads=[USB, B("par")], writes=[TB[7]])
                        pg.op(dve, lambda e, us=us, cvs=cvs, w1=w1: e.scalar_tensor_tensor(out=cvs, in0=us[:, :, 1:9], scalar=w1, in1=cvs, op0=ALU.mult, op1=ALU.add),
                              reads=[USB, TB[7]], writes=[TB[7]])
                        pg.op(dve, lambda e, us=us, cvs=cvs, w0=w0: e.scalar_tensor_tensor(out=cvs, in0=us[:, :, 0:8], scalar=w0, in1=cvs, op0=ALU.mult, op1=ALU.add),
                              reads=[USB, TB[7]], writes=[TB[7]])
                        pg.op(dve, lambda e, bA=bA, npc=npc, n=n, c=c, c0=c0: e.tensor_tensor(out=WB[:, 4 + c, c0 + npc:c0 + n], in0=ps[:, bA, npc:n], in1=tmp[7][:, 0:128], op=ALU.mult),
                              reads=[PSB[bA], TB[7]], writes=[B("WB", 4 + c, si)])
                W.release(3)
            if sample:
                for c in range(4):
                    pg.op(dve, lambda e, c=c: e.tensor_copy(tmp[6][:, c * 32:(c + 1) * 32].rearrange("p (i t) -> p i t", t=2), usmp[:, c, :, 8:10]),
                          reads=[B("usmp", c)], writes=[TB[6]])
                pg.group(pe, [lambda e, c=c: e.transpose(ps[0:32, 7, c * 128:(c + 1) * 128], tmp[6][:, c * 32:(c + 1) * 32], ident) for c in range(4)],
                         reads=[TB[6], CST], writes=[PSB[7]])
                pg.op(dve, lambda e: e.tensor_copy(sco, ps[0:32, 7, :]), reads=[PSB[7]], writes=[TB[7]])
                pg.dma(sp, osem, lambda e: e.dma_start(out=scs_d, in_=sco), reads=[TB[7]])
            if last_pass:
                pg.group(pe, [lambda e, c=c: e.transpose(ps[0:2, 7, c * 128:(c + 1) * 128], ucar[:, c, :], ident) for c in range(4)],
                         reads=[B("ucar", c) for c in range(4)] + [CST], writes=[PSB[7]])
                pg.op(dve, lambda e: e.tensor_copy(tmp[7][0:2, 0:512], ps[0:2, 7, :]), reads=[PSB[7]], writes=[TB[7]])
                pg.dma(sp, osem, lambda e: e.dma_start(out=scp_d, in_=tmp[7][0:2, 0:512]), reads=[TB[7]])

            if stage < 2.4:
                return
            it4 = 0
            ity = 0
            for qt in range(4):
                for mm_ in range(2):
                    m = qt * 2 + mm_
                    wga, wgab = W.get(("wga", m))
                    wgb, wgbb = W.get(("wgb", m))
                    wab, wabb = W.get(("wab", m))
                    wao, waob = wab[:, 0:4, :], wabb
                    wbo, wbob = wab[:, 4:8, :], wabb
                    for si, (c0, n) in enumerate(subs):
                        it4 += 1
                        st_ = it4 % 2
                        bk = [0, 1, 2, 3] if st_ == 0 else [4, 5, 6, 7]
                        proj(wga, wgab, bk[0], si, c0, n)
                        proj(wgb, wgbb, bk[1], si, c0, n)
                        pg.group(pe, [mm(ps[:, bk[2], 0:n], wao[:, kk, :], WB[:, kk, c0:c0 + n], kk == 0, kk == 3) for kk in range(4)],
                                 reads=[waob] + [B("WB", kk, si) for kk in range(4)], writes=[PSB[bk[2]]])
                        pg.group(pe, [mm(ps[:, bk[3], 0:n], wbo[:, kk, :], WB[:, 4 + kk, c0:c0 + n], kk == 0, kk == 3) for kk in range(4)],
                                 reads=[wbob] + [B("WB", 4 + kk, si) for kk in range(4)], writes=[PSB[bk[3]]])
                        ta, TAB = tmp[st_], TB[st_]
                        tb, TBB = tmp[2 + st_], TB[2 + st_]
                        t1, T1B = tmp[4 + st_], TB[4 + st_]
                        t2, T2B = tmp[6 + st_], TB[6 + st_]
                        pg.op(act, lambda e, ta=ta, b0=bk[0], n=n: e.activation(out=ta[:, 0:n], in_=ps[:, b0, 0:n], func=AF.Tanh, scale=0.5),
                              reads=[PSB[bk[0]]], writes=[TAB])
                        pg.op(act, lambda e, tb=tb, b1=bk[1], n=n: e.activation(out=tb[:, 0:n], in_=ps[:, b1, 0:n], func=AF.Tanh, scale=0.5),
                              reads=[PSB[bk[1]]], writes=[TBB])
                        pg.op(dve, lambda e, ta=ta, t1=t1, b2=bk[2], n=n: e.scalar_tensor_tensor(out=t1[:, 0:n], in0=ta[:, 0:n], scalar=1.0, in1=ps[:, b2, 0:n],
                                                                                              op0=ALU.add, op1=ALU.mult),
                              reads=[TAB, PSB[bk[2]]], writes=[T1B])
                        pg.op(dve, lambda e, tb=tb, t2=t2, b3=bk[3], n=n: e.scalar_tensor_tensor(out=t2[:, 0:n], in0=tb[:, 0:n], scalar=1.0, in1=ps[:, b3, 0:n],
                                                                                              op0=ALU.add, op1=ALU.mult),
                              reads=[TBB, PSB[bk[3]]], writes=[T2B])
                        pg.op(dve, lambda e, t1=t1, t2=t2, mm_=mm_, c0=c0, n=n: e.tensor_tensor(out=WB[:, 8 + mm_, c0:c0 + n], in0=t1[:, 0:n], in1=t2[:, 0:n], op=ALU.add),
                              reads=[T1B, T2B], writes=[B("WB", 8 + mm_, si)])
                    W.release(3)
                for mo in range(8):
                    wo_, wob = W.get(("wo", qt, mo))
                    for si, (c0, n) in enumerate(subs):
                        ity += 1
                        by = [0, 4, 1, 5][ity % 4]
                        pg.group(pe, [mm(ps[:, by, 0:n], wo_[:, kk, :], WB[:, 8 + kk, c0:c0 + n], kk == 0, kk == 1) for kk in range(2)],
                                 reads=[wob] + [B("WB", 8 + kk, si) for kk in range(2)], writes=[PSB[by]])
                        xupdate(ip, mo, si, c0, n, by, 0.5)
                    W.release(1)

        if PASSES[0]["sample"]:
            load_sample_conv_state()
        for ip in range(len(PASSES)):
            load_x(ip)
            if ip == 0:
                W = WRing(pg, arena, wsems, plan)
            if stage >= 1:
                norm(ip, PG1)
                ffn(ip, "a")
            if stage >= 2:
                mixer(ip)
            if stage >= 3:
                norm(ip, PG2)
                ffn(ip, "b")
            final(ip)

        for ds_ in ysem + [osem, snsem] + snsem2:
            if ds_.count:
                sp.wait(Tok(ds_.sem, ds_.count, None))
        pg.replay(block)
    return nc


def _consts():
    c = np.zeros((128, NCST), np.float32)
    idx = np.arange(128)
    c[:, C_ID:C_ID + 128] = np.eye(128, dtype=np.float32)
    c[:, C_ONE:C_ONE + 128] = 1.0
    s, t = idx[:, None], idx[None, :]
    c[:, C_MP:C_MP + 128] = ((s // 64 == t // 64) & (s <= t)).astype(np.float32)
    c[:, C_MS:C_MS + 128] = ((s // 8 == t // 8) & (s <= t)).astype(np.float32)
    r64 = (np.arange(512) % 64 == 0).astype(np.float32)
    c[:, C_R64:C_R64 + 512] = r64[None, :]
    rmix = np.concatenate([(np.arange(256) % 64 == 0), (np.arange(128) % 8 == 0)]).astype(np.float32)
    c[:, C_RMIX:C_RMIX + 384] = rmix[None, :]
    c[:, C_M2:C_M2 + 16] = (idx[:, None] // 8 == np.arange(16)[None, :]).astype(np.float32)
    c[:, C_CM2:C_CM2 + 2] = (idx[:, None] // 64 == np.arange(2)[None, :]).astype(np.float32)
    return c


def _fm(v, nchunk):
    return np.ascontiguousarray(np.asarray(v, np.float32).reshape(nchunk, 128).T)


_PROG_CACHE = {}


def kernel(x_prompt, x_sample, state_hgrn, state_conv, lower_bound_logits, g_ffn1, w1_ffn1, w3_ffn1, w2_ffn1,
           g_mix, w_in, conv_w, g_hgrn_out, w_a_out, w_b_out, w_o, g_ffn2, w1_ffn2, w3_ffn2, w2_ffn2, g_final,
           _stage=99):
    f32 = np.float32
    par = np.zeros((128, NPAR), f32)
    par[:, PG1:PG1 + 8] = _fm(g_ffn1[0], 8)
    par[:, PGM:PGM + 8] = _fm(g_mix[0], 8)
    par[:, PG2:PG2 + 8] = _fm(g_ffn2[0], 8)
    par[:, PGF:PGF + 8] = _fm(g_final, 8)
    par[:, PL0:PL0 + 4] = _fm(lower_bound_logits[0], 4)
    par[:, PL1:PL1 + 4] = _fm(lower_bound_logits[1], 4)
    par[:, PGH:PGH + 4] = _fm(g_hgrn_out[0], 4)
    for c in range(4):
        for j in range(3):
            par[:, PCW + c * 3 + j] = np.asarray(conv_w[0, j, c * 128:(c + 1) * 128], f32)
    cst = _consts()
    shared = dict(par=par, cst=cst,
                  w1a=np.ascontiguousarray(w1_ffn1[0], f32), w3a=np.ascontiguousarray(w3_ffn1[0], f32),
                  w2a=np.ascontiguousarray(w2_ffn1[0], f32), win=np.ascontiguousarray(w_in[0], f32),
                  wab=np.ascontiguousarray(np.concatenate([w_a_out[0], w_b_out[0]], axis=0), f32),
                  wo=np.ascontiguousarray(w_o[0], f32),
                  w1b=np.ascontiguousarray(w1_ffn2[0], f32), w3b=np.ascontiguousarray(w3_ffn2[0], f32),
                  w2b=np.ascontiguousarray(w2_ffn2[0], f32))
    in_maps = []
    for c in range(NCORES):
        m = dict(shared)
        m["xp"] = np.ascontiguousarray(x_prompt[c], f32)
        m["xs"] = np.ascontiguousarray(x_sample[c * 16:(c + 1) * 16], f32).reshape(128, D)
        m["sh"] = np.ascontiguousarray(state_hgrn[0, c * 16:(c + 1) * 16], f32)
        m["sc"] = np.ascontiguousarray(state_conv[0, c * 16:(c + 1) * 16], f32).reshape(32, 512)
        in_maps.append(m)
    if _stage not in _PROG_CACHE:
        _PROG_CACHE[_stage] = build_program(_stage)
    nc = _PROG_CACHE[_stage]
    res = run_bass_kernel_spmd(nc, in_maps, core_ids=list(range(NCORES)))
    r = res.results
    y_prompt = np.stack([r[c]["yp"] for c in range(NCORES)], 0)
    y_sample = np.concatenate([r[c]["ys"].reshape(16, 8, D) for c in range(NCORES)], 0)
    shp = np.stack([r[c]["shp"] for c in range(NCORES)], 0)[None]
    scp = np.stack([r[c]["scp"] for c in range(NCORES)], 0)[None]
    shs = np.concatenate([r[c]["shs"] for c in range(NCORES)], 0)[None]
    scs = np.concatenate([r[c]["scs"].reshape(16, 2, 512) for c in range(NCORES)], 0)[None]
    return (y_prompt.astype(f32), y_sample.astype(f32), shp.astype(f32), scp.astype(f32), shs.astype(f32), scs.astype(f32))
```
